# Optimizing a Trainium2 kernel written in Bass

```python
import jax, jax.numpy as jnp
from jax import lax
import numpy as np


D_MODEL = 1024
BATCH = 4
SEQ = 4096
DEPTH = 4

GRID_W = 64
CTX_LEN = 256
HEAD_DIM = 64
N_Q_HEADS = 16
N_KV_HEADS = 4
Q_PER_KV = N_Q_HEADS // N_KV_HEADS
D_ATTN = N_Q_HEADS * HEAD_DIM
D_KV = N_KV_HEADS * HEAD_DIM
D_LRU = D_MODEL
LRU_BLOCKS = 16
LRU_BLOCK = D_LRU // LRU_BLOCKS
CONV_W = 4
CONV_PAD_LEFT = 2
LRU_C = 8.0
ROPE_THETA = 10000.0
Q_BLOCK = 128
EPS = 1e-6
D_IN = 2 * D_ATTN + 2 * D_KV + 2 * D_LRU + 2 * D_MODEL

kernel_name = "hybrid_gqa_rglru_dit_trunk"


def rmsnorm(u, g):
    u32 = u.astype(jnp.float32)
    y = u32 * lax.rsqrt(jnp.mean(u32 * u32, axis=-1, keepdims=True) + EPS)
    return (y * g.astype(jnp.float32)).astype(u.dtype)


def modulation(cond, w_mod, b_mod):
    m = jax.nn.silu(cond) @ w_mod + b_mod
    shift, scale, gate = jnp.split(m, 3, axis=-1)
    if cond.ndim == 2:
        shift, scale, gate = shift[:, None, :], scale[:, None, :], gate[:, None, :]
    return shift, scale, gate


def split_proj(p):
    sizes = (D_ATTN, D_KV, D_KV, D_ATTN, D_LRU, D_LRU, 2 * D_MODEL)
    points = [sum(sizes[:i + 1]) for i in range(len(sizes) - 1)]
    return jnp.split(p, points, axis=-1)


def axial_rope(u, rows, cols):
    half = HEAD_DIM // 2
    quarter = half // 2
    freqs = ROPE_THETA ** (-jnp.arange(quarter, dtype=jnp.float32) / quarter)

    def rot(seg, pos):
        ang = pos.astype(jnp.float32)[:, None] * freqs
        cos = jnp.cos(ang)[None, :, None, :]
        sin = jnp.sin(ang)[None, :, None, :]
        s = seg.astype(jnp.float32)
        s1, s2 = s[..., :quarter], s[..., quarter:]
        return jnp.concatenate([s1 * cos - s2 * sin, s2 * cos + s1 * sin], axis=-1)

    out = jnp.concatenate([rot(u[..., :half], rows), rot(u[..., half:], cols)], axis=-1)
    return out.astype(u.dtype)


def gqa(q, k, v):
    s = jnp.einsum('bqkgd,bskd->bkgqs', q, k).astype(jnp.float32)
    p = jax.nn.softmax(s, axis=-1).astype(v.dtype)
    return jnp.einsum('bkgqs,bskd->bqkgd', p, v)


def latent_attention(q, k_all, v_all):
    B, n = q.shape[0], q.shape[1]
    nb = n // Q_BLOCK
    qb = q.reshape(B, nb, Q_BLOCK, N_KV_HEADS, Q_PER_KV, HEAD_DIM).swapaxes(0, 1)
    o = lax.map(lambda qblk: gqa(qblk, k_all, v_all), qb)
    return o.swapaxes(0, 1).reshape(B, n, D_ATTN)


def centred_conv(u, w, b):
    n = u.shape[1]
    up = jnp.pad(u, ((0, 0), (CONV_PAD_LEFT, CONV_W - 1 - CONV_PAD_LEFT), (0, 0)))
    out = up[:, 0:n] * w[0]
    for j in range(1, CONV_W):
        out = out + up[:, j:j + n] * w[j]
    return out + b


def block_diag(u, w):
    ub = u.reshape(u.shape[:-1] + (LRU_BLOCKS, LRU_BLOCK))
    return jnp.einsum('bnhi,hij->bnhj', ub, w).reshape(u.shape)


def lru_coeffs(u, w_gates, b_gates, lam):
    u32 = u.astype(jnp.float32)
    r = jax.nn.sigmoid(block_diag(u32, w_gates[0].astype(jnp.float32)) + b_gates[0].astype(jnp.float32))
    i = jax.nn.sigmoid(block_diag(u32, w_gates[1].astype(jnp.float32)) + b_gates[1].astype(jnp.float32))
    log_a = -LRU_C * r * jax.nn.softplus(-lam.astype(jnp.float32))
    a = jnp.exp(log_a)
    b = jnp.sqrt(-jnp.expm1(2.0 * log_a)) * (i * u32)
    return a, b


def _scan_combine(left, right):
    a1, b1 = left
    a2, b2 = right
    return a1 * a2, a2 * b1 + b2


def linear_scan(a, b, h0, reverse):
    if h0 is not None:
        idx = -1 if reverse else 0
        b = b.at[:, idx].add(a[:, idx] * h0)
    _, h = lax.associative_scan(_scan_combine, (a, b), reverse=reverse, axis=1)
    return h


def lru_branch(u_ctx, u_lat, conv_w, conv_b, w_gates, b_gates, lam, need_ctx):
    uc = centred_conv(u_ctx, conv_w, conv_b)
    ul = centred_conv(u_lat, conv_w, conv_b)
    h_ctx_dirs, h_lat_dirs = [], []
    for d, rev in enumerate((False, True)):
        a_c, b_c = lru_coeffs(uc, w_gates[d], b_gates[d], lam[d])
        h_c = linear_scan(a_c, b_c, None, rev)
        h_end = h_c[:, 0] if rev else h_c[:, -1]
        a_l, b_l = lru_coeffs(ul, w_gates[d], b_gates[d], lam[d])
        h_lat_dirs.append(linear_scan(a_l, b_l, h_end, rev))
        h_ctx_dirs.append(h_c)
    y_lat = (h_lat_dirs[0] + h_lat_dirs[1]).astype(u_lat.dtype)
    y_ctx = (h_ctx_dirs[0] + h_ctx_dirs[1]).astype(u_ctx.dtype) if need_ctx else None
    return y_ctx, y_lat


def merge_branches(o_attn, g_attn, o_lru, g_lru, g_merge, w_a_out, w_b_out, w_out):
    ya = (o_attn * jax.nn.silu(g_attn)) @ w_a_out
    yb = (o_lru * jax.nn.silu(g_lru)) @ w_b_out
    gma, gmb = jnp.split(g_merge, 2, axis=-1)
    return (jax.nn.sigmoid(gma) * ya + jax.nn.sigmoid(gmb) * yb) @ w_out


def setup_inputs(seed: int = 0) -> dict:
    key = jax.random.key(seed)
    ks = jax.random.split(key, 20)

    def nrm(k, shape, s):
        return jax.random.normal(k, shape, jnp.float32) * s

    a0 = jax.random.uniform(ks[14], (DEPTH, 2, D_LRU), jnp.float32, minval=0.9, maxval=0.999)
    return {
        "x": nrm(ks[0], (BATCH, SEQ, D_MODEL), 1.0),
        "c": nrm(ks[1], (BATCH, D_MODEL), 1.0),
        "ctx": nrm(ks[2], (BATCH, CTX_LEN, D_MODEL), 1.0),
        "c_ctx": nrm(ks[3], (D_MODEL,), 1.0),
        "norm_g": 1.0 + nrm(ks[4], (DEPTH, D_MODEL), 0.05),
        "w_mod": nrm(ks[5], (DEPTH, D_MODEL, 3 * D_MODEL), 0.5 * D_MODEL ** -0.5),
        "b_mod": nrm(ks[6], (DEPTH, 3 * D_MODEL), 0.01),
        "w_in": nrm(ks[7], (DEPTH, D_MODEL, D_IN), D_MODEL ** -0.5),
        "q_norm_g": 1.0 + nrm(ks[8], (DEPTH, HEAD_DIM), 0.05),
        "k_norm_g": 1.0 + nrm(ks[9], (DEPTH, HEAD_DIM), 0.05),
        "conv_w": nrm(ks[10], (DEPTH, CONV_W, D_LRU), CONV_W ** -0.5),
        "conv_b": nrm(ks[11], (DEPTH, D_LRU), 0.01),
        "lru_gate_w": nrm(ks[12], (DEPTH, 2, 2, LRU_BLOCKS, LRU_BLOCK, LRU_BLOCK), LRU_BLOCK ** -0.5),
        "lru_gate_b": nrm(ks[13], (DEPTH, 2, 2, D_LRU), 0.01),
        "lru_lambda": jnp.log(a0) - jnp.log1p(-a0),
        "w_a_out": nrm(ks[15], (DEPTH, D_ATTN, D_MODEL), D_ATTN ** -0.5),
        "w_b_out": nrm(ks[16], (DEPTH, D_LRU, D_MODEL), D_LRU ** -0.5),
        "w_out": nrm(ks[17], (DEPTH, D_MODEL, D_MODEL), D_MODEL ** -0.5),
    }


def reference(x, c, ctx, c_ctx, norm_g, w_mod, b_mod, w_in, q_norm_g, k_norm_g, conv_w, conv_b,
              lru_gate_w, lru_gate_b, lru_lambda, w_a_out, w_b_out, w_out):
    B, n, _ = x.shape
    L = ctx.shape[1]
    ROWS = n // GRID_W
    rows = jnp.broadcast_to(jnp.arange(ROWS)[:, None], (ROWS, GRID_W)).reshape(-1)
    cols = jnp.broadcast_to(jnp.arange(GRID_W)[None, :], (ROWS, GRID_W)).reshape(-1)
    scale = HEAD_DIM ** -0.5

    for l in range(DEPTH):
        need_ctx = l < DEPTH - 1
        sh_l, sc_l, gt_l = modulation(c, w_mod[l], b_mod[l])
        sh_c, sc_c, gt_c = modulation(c_ctx, w_mod[l], b_mod[l])
        h_lat = rmsnorm(x, norm_g[l]) * (1.0 + sc_l) + sh_l
        h_ctx = rmsnorm(ctx, norm_g[l]) * (1.0 + sc_c) + sh_c

        q_l, k_l, v_l, ga_l, u_l, gb_l, gm_l = split_proj(h_lat @ w_in[l])
        q_c, k_c, v_c, ga_c, u_c, gb_c, gm_c = split_proj(h_ctx @ w_in[l])

        q_l = axial_rope(rmsnorm(q_l.reshape(B, n, N_Q_HEADS, HEAD_DIM), q_norm_g[l]), rows, cols) * scale
        k_l = axial_rope(rmsnorm(k_l.reshape(B, n, N_KV_HEADS, HEAD_DIM), k_norm_g[l]), rows, cols)
        v_l = v_l.reshape(B, n, N_KV_HEADS, HEAD_DIM)
        k_c = rmsnorm(k_c.reshape(B, L, N_KV_HEADS, HEAD_DIM), k_norm_g[l])
        v_c = v_c.reshape(B, L, N_KV_HEADS, HEAD_DIM)
        k_all = jnp.concatenate([k_c, k_l], axis=1)
        v_all = jnp.concatenate([v_c, v_l], axis=1)
        o_lat = latent_attention(q_l, k_all, v_all)

        lru_c, lru_l = lru_branch(u_c, u_l, conv_w[l], conv_b[l], lru_gate_w[l], lru_gate_b[l],
                                  lru_lambda[l], need_ctx)

        x = x + gt_l * merge_branches(o_lat, ga_l, lru_l, gb_l, gm_l, w_a_out[l], w_b_out[l], w_out[l])

        if need_ctx:
            q_c = rmsnorm(q_c.reshape(B, L, N_Q_HEADS, HEAD_DIM), q_norm_g[l]) * scale
            o_ctx = gqa(q_c.reshape(B, L, N_KV_HEADS, Q_PER_KV, HEAD_DIM), k_c, v_c).reshape(B, L, D_ATTN)
            ctx = ctx + gt_c * merge_branches(o_ctx, ga_c, lru_c, gb_c, gm_c, w_a_out[l], w_b_out[l], w_out[l])

    return x
```

```python
import numpy as np
import concourse.bass as bass
import concourse.mybir as mybir
from concourse.bass_utils import run_bass_kernel_spmd
from contextlib import ExitStack

F32 = mybir.dt.float32
BF16 = mybir.dt.bfloat16
ALU = mybir.AluOpType
AF = mybir.ActivationFunctionType

ENGS = ("pe", "act", "dve", "pool", "sp")

D = 1024
NCTX = 256
NLAT = 4096
T = NCTX + NLAT
NBLK = T // 128
HD = 64
EPS = 1e-6
DEPTH = 4


class Buf:
    __slots__ = ("name", "w", "r", "lsem", "ssem")

    def __init__(self, name=""):
        self.name = name
        self.w = None
        self.r = []
        self.lsem = None
        self.ssem = None


class TB:
    __slots__ = ("t", "b")

    def __init__(self, t, name=""):
        self.t = t
        self.b = Buf(name)


class Rot:
    def __init__(self, items):
        self.items = items
        self.i = 0

    def next(self):
        x = self.items[self.i % len(self.items)]
        self.i += 1
        return x


class Sched:
    def __init__(self, nc, stack, same_engine_sync=True):
        self.nc = nc
        self.stack = stack
        self.same = same_engine_sync
        self.esem = {e: stack.enter_context(nc.semaphore("es_" + e)) for e in ENGS if e != "sp"}
        self.cnt = {e: 0 for e in ENGS}
        self.seen = {e: {} for e in ENGS}
        self.prog = {e: [] for e in ENGS}
        self.free_dsems = []
        self.all_dsems = []
        self.n_dsem = 0
        self.dma_bufs = []
        self.n_instr = 0

    def _get_dsem(self, Q="sp"):
        kind = "sw" if Q == "pool" else "hw"
        for i, s in enumerate(self.free_dsems):
            if s[2] == kind:
                return self.free_dsems.pop(i)
        s = [self.stack.enter_context(self.nc.semaphore("ds%d" % self.n_dsem)), 0, kind]
        self.n_dsem += 1
        self.all_dsems.append(s)
        return s

    def _waits_for(self, E, reads, writes):
        evs = []
        for b in reads:
            if b.w is not None:
                evs.append(b.w)
        for b in writes:
            if b.w is not None:
                evs.append(b.w)
            evs.extend(b.r)
        out = []
        seen = self.seen[E]
        for ev in evs:
            if ev[0] == "e":
                _, X, k = ev
                if X == E and (not self.same or E == "pe"):
                    continue
                key = ("e", X)
                sem = self.esem[X]
            else:
                _, s, k = ev
                key = ("d", id(s))
                sem = s[0]
            if seen.get(key, 0) >= k:
                continue
            seen[key] = k
            out.append((sem, k))
        return out

    def op(self, E, fn, reads=(), writes=()):
        waits = self._waits_for(E, reads, writes)
        self.cnt[E] += 1
        ev = ("e", E, self.cnt[E])
        for b in reads:
            b.r.append(ev)
        for b in writes:
            b.w = ev
            b.r = []
        self.prog[E].append((waits, fn, (self.esem[E], 1)))
        self.n_instr += 1

    def coll(self, fn, reads=(), writes=()):
        waits = self._waits_for("pool", reads, writes)
        if not hasattr(self, "ccsem"):
            self.ccsem = [self.stack.enter_context(self.nc.semaphore("ccsem")), 0, "cc"]
            self.all_dsems.append(self.ccsem)
        s = self.ccsem
        s[1] += 1
        ev = ("d", s, s[1])
        for b in reads:
            b.r.append(ev)
        for b in writes:
            b.w = ev
            b.r = []
        self.prog["pool"].append((waits, fn, (s[0], 1)))
        self.n_instr += 1

    def dma(self, fn, reads=(), writes=(), Q="sp", dram_reads=(), dram_writes=()):
        waits = self._waits_for(Q, list(reads) + list(dram_reads), list(writes) + list(dram_writes))
        if writes:
            b = writes[0]
            assert len(writes) == 1 and not reads
            if b.lsem is None or b.lsem[2] != ("sw" if Q == "pool" else "hw"):
                assert b.lsem is None
                b.lsem = self._get_dsem(Q)
                self.dma_bufs.append(b)
            s = b.lsem
        else:
            b = reads[0]
            assert len(reads) == 1
            if b.ssem is None:
                b.ssem = self._get_dsem(Q)
                self.dma_bufs.append(b)
            assert b.ssem[2] == ("sw" if Q == "pool" else "hw")
            s = b.ssem
        s[1] += 16
        ev = ("d", s, s[1])
        for b in list(reads) + list(dram_reads):
            b.r.append(ev)
        for b in list(writes) + list(dram_writes):
            b.w = ev
            b.r = []
        self.prog[Q].append((waits, fn, (s[0], 16)))
        self.n_instr += 1

    def barrier(self):
        for E in ENGS:
            waits = []
            seen = self.seen[E]
            for X in ENGS:
                if X == E or X == "sp" or self.cnt[X] == 0:
                    continue
                key = ("e", X)
                if seen.get(key, 0) < self.cnt[X]:
                    seen[key] = self.cnt[X]
                    waits.append((self.esem[X], self.cnt[X]))
            for s in self.all_dsems:
                key = ("d", id(s))
                if s[1] > 0 and seen.get(key, 0) < s[1]:
                    seen[key] = s[1]
                    waits.append((s[0], s[1]))
            if waits:
                self.prog[E].append((waits, None, None))

    def end_phase(self):
        self.barrier()
        self.emit()
        for b in self.dma_bufs:
            for a in ("lsem", "ssem"):
                s = getattr(b, a)
                if s is not None:
                    self.free_dsems.append(s)
                    setattr(b, a, None)
        self.dma_bufs = []

    def emit(self):
        nc = self.nc
        prog = self.prog

        def run(E, eng):
            for waits, fn, inc in prog[E]:
                for sem, v in waits:
                    eng.wait_ge(sem, v)
                if fn is not None:
                    ins = fn(eng)
                    ins.then_inc(inc[0], inc[1])

        with nc.Block() as block:
            @block.sync
            def _(eng):
                run("sp", eng)

            @block.tensor
            def _(eng):
                run("pe", eng)

            @block.scalar
            def _(eng):
                run("act", eng)

            @block.vector
            def _(eng):
                run("dve", eng)

            @block.gpsimd
            def _(eng):
                run("pool", eng)
        self.prog = {e: [] for e in ENGS}


class Cfg:
    def __init__(self, L=DEPTH, split=1, dbg=False, total_depth=DEPTH):
        self.L = L
        self.split = split
        self.NQC = 8 // split
        self.NKC = 2 // split
        self.NLC = 8 // split
        self.NV = self.NKC * 128
        self.dbg = dbg
        self.total_depth = total_depth
        ch = []
        ch += [("q", j) for j in range(self.NQC)]
        ch += [("k", j) for j in range(self.NKC)]
        ch += [("ga", j) for j in range(self.NQC)]
        ch += [("u", j) for j in range(self.NLC)]
        ch += [("gb", j) for j in range(self.NLC)]
        ch += [("gm", j) for j in range(16)]
        ch += [("v", j) for j in range(self.NKC)]
        self.chunks = ch
        self.NCOLS = len(ch) * 128
        o = 0
        self.V_NG = o; o += 8
        self.V_BM = o; o += 24
        self.V_QG = o; o += 1
        self.V_KG = o; o += 1
        self.V_CW = o; o += 4 * self.NLC
        self.V_CB = o; o += self.NLC
        self.V_GB = o; o += 4 * self.NLC
        self.V_LM = o; o += 2 * self.NLC
        self.VL = o
        self.V_C = self.VL * L
        self.V_CX = self.V_C + 8
        self.NVEC = self.V_CX + 8

    def qheads(self, half):
        out = []
        if self.split == 1:
            for j in range(8):
                grp = (j // 4) * 2
                out.append((grp * 4 + j % 4, (grp + 1) * 4 + j % 4))
        else:
            for j in range(4):
                out.append((half * 8 + j, half * 8 + 4 + j))
        return out


def chunk_major(v, n):
    return np.ascontiguousarray(np.asarray(v, np.float32).reshape(n, 128).T)


def rope_tables():
    quarter = 16
    freqs = (10000.0 ** (-np.arange(quarter, dtype=np.float32) / quarter)).astype(np.float32)
    t = np.arange(NLAT)
    rows = (t // 64).astype(np.float32)
    cols = (t % 64).astype(np.float32)
    cosT = np.zeros((128, NLAT), np.float32)
    sinT = np.zeros((128, NLAT), np.float32)
    for p in range(128):
        d = p % 64
        pos = rows if d < 32 else cols
        dd = d % 32
        i = dd % 16
        ang = (pos * freqs[i]).astype(np.float32)
        cosT[p] = np.cos(ang)
        sinT[p] = (-np.sin(ang)) if dd < 16 else np.sin(ang)
    return np.stack([cosT, sinT])


def const_mats():
    ident = np.eye(128, dtype=np.float32)
    bones = np.zeros((128, 128), np.float32)
    bones[:64, :64] = 1.0
    bones[64:, 64:] = 1.0
    rm = np.zeros((128, 128), np.float32)
    for m in range(128):
        d = m % 64
        base = m - d
        dd = d % 32
        partner = d + 16 if dd < 16 else d - 16
        rm[base + partner, m] = 1.0
    return np.stack([ident, bones, rm])


def prep_core_inputs(cfg, inp, b, half):
    L = cfg.L
    qh = cfg.qheads(half)
    qcols = []
    for lo, hi in qh:
        qcols += list(range(lo * 64, lo * 64 + 64)) + list(range(hi * 64, hi * 64 + 64))
    qcols = np.array(qcols)
    if cfg.split == 1:
        kvh = [0, 1, 2, 3]
        lch = np.arange(1024)
    else:
        kvh = [2 * half, 2 * half + 1]
        lch = np.arange(512) + 512 * half
    kcols = np.concatenate([np.arange(h * 64, h * 64 + 64) for h in kvh])
    o_q, o_k, o_v, o_ga, o_u, o_gb, o_gm = 0, 1024, 1280, 1536, 2560, 3584, 4608
    cols = np.concatenate([o_q + qcols, o_k + kcols, o_ga + qcols, o_u + lch, o_gb + lch,
                           o_gm + np.arange(2048), o_v + kcols])
    assert len(cols) == cfg.NCOLS
    w_in = np.ascontiguousarray(np.asarray(inp["w_in"])[:L][:, :, cols])
    w_a = np.ascontiguousarray(np.asarray(inp["w_a_out"])[:L][:, qcols, :])
    w_b = np.ascontiguousarray(np.asarray(inp["w_b_out"])[:L][:, lch, :])
    w_o = np.ascontiguousarray(np.asarray(inp["w_out"])[:L])
    w_mod = np.ascontiguousarray(np.asarray(inp["w_mod"])[:L])
    nlc = cfg.NLC
    vecs = np.zeros((128, cfg.NVEC), np.float32)
    for l in range(L):
        o = l * cfg.VL
        vecs[:, o + cfg.V_NG:o + cfg.V_NG + 8] = chunk_major(inp["norm_g"][l], 8)
        vecs[:, o + cfg.V_BM:o + cfg.V_BM + 24] = chunk_major(inp["b_mod"][l], 24)
        vecs[:, o + cfg.V_QG] = np.tile(np.asarray(inp["q_norm_g"][l]), 2)
        vecs[:, o + cfg.V_KG] = np.tile(np.asarray(inp["k_norm_g"][l]), 2)
        for j in range(4):
            vecs[:, o + cfg.V_CW + j * nlc:o + cfg.V_CW + (j + 1) * nlc] = chunk_major(np.asarray(inp["conv_w"])[l, j][lch], nlc)
        vecs[:, o + cfg.V_CB:o + cfg.V_CB + nlc] = chunk_major(np.asarray(inp["conv_b"])[l][lch], nlc)
        for d in range(2):
            for g in range(2):
                k0 = o + cfg.V_GB + (d * 2 + g) * nlc
                vecs[:, k0:k0 + nlc] = chunk_major(np.asarray(inp["lru_gate_b"])[l, d, g][lch], nlc)
            k0 = o + cfg.V_LM + d * nlc
            vecs[:, k0:k0 + nlc] = chunk_major(np.asarray(inp["lru_lambda"])[l, d][lch], nlc)
    vecs[:, cfg.V_C:cfg.V_C + 8] = chunk_major(np.asarray(inp["c"])[b], 8)
    vecs[:, cfg.V_CX:cfg.V_CX + 8] = chunk_major(np.asarray(inp["c_ctx"]), 8)
    gw = np.asarray(inp["lru_gate_w"])[:L]
    if cfg.split == 2:
        gw = gw[:, :, :, 8 * half:8 * half + 8]
    gw = np.ascontiguousarray(gw)
    xc = np.ascontiguousarray(np.concatenate([np.asarray(inp["ctx"])[b], np.asarray(inp["x"])[b]], axis=0))
    return {"xc": xc, "w_in": w_in, "w_mod": w_mod, "w_a": w_a, "w_b": w_b, "w_o": w_o,
            "vecs": vecs, "gate_w": gw, "rope": rope_tables(), "cmats": const_mats()}


def build_program(cfg):
    nc = bass.Bass("TRN2", target_bir_lowering=False)
    L, NQC, NKC, NLC, NV = cfg.L, cfg.NQC, cfg.NKC, cfg.NLC, cfg.NV
    NCOLS = cfg.NCOLS

    def dram_in(name, shape, dt=F32):
        return nc.dram_tensor(name, list(shape), dt, kind="ExternalInput").ap()

    skind = "ExternalOutput" if cfg.dbg else "Internal"

    def dram_scr(name, shape, dt):
        return nc.dram_tensor(name, list(shape), dt, kind=skind).ap()

    xc_d = dram_in("xc", [T, D])
    w_in_d = dram_in("w_in", [L, D, NCOLS])
    w_mod_d = dram_in("w_mod", [L, D, 3 * D])
    w_a_d = dram_in("w_a", [L, NQC * 128, D])
    w_b_d = dram_in("w_b", [L, NLC * 128, D])
    w_o_d = dram_in("w_o", [L, D, D])
    vecs_d = dram_in("vecs", [128, cfg.NVEC])
    gate_w_d = dram_in("gate_w", [L, 2, 2, 2 * NLC, 64, 64])
    rope_d = dram_in("rope", [2, 128, NLAT])
    cmats_d = dram_in("cmats", [3, 128, 128])
    out_d = nc.dram_tensor("out", [NLAT, D], F32, kind="ExternalOutput").ap()

    X_s = dram_scr("X_s", [T, D], F32)
    QT_s = dram_scr("QT_s", [NQC, 128, T], BF16)
    KT_s = dram_scr("KT_s", [NKC, 128, T], BF16)
    V_s = dram_scr("V_s", [T, NV], BF16)
    SGA_s = dram_scr("SGA_s", [NQC, 128, T], BF16)
    UT_s = dram_scr("UT_s", [NLC, 128, T], F32)
    SGB_s = dram_scr("SGB_s", [NLC, 128, T], BF16)
    SGM_s = dram_scr("SGM_s", [16, 128, T], BF16)
    OG_s = dram_scr("OG_s", [NQC, 128, T], BF16)
    HG_s = dram_scr("HG_s", [NLC, 128, T], BF16)
    if cfg.split == 2:
        Zp_s = nc.dram_tensor("Zp_s", [T, D], F32).ap()
        Zr_s = nc.dram_tensor("Zr_s", [NBLK, 256, D], F32).ap()
    PAIRS = [[0, 1], [2, 3], [4, 5], [6, 7]]

    tiles = [(0, NCTX, 1)] + [(NCTX + 512 * i, 512, 0) for i in range(NLAT // 512)]

    with ExitStack() as top:
        S = Sched(nc, top)

        uid = [0]

        def un(name):
            uid[0] += 1
            return "%s_%d" % (name, uid[0])

        def sb(stack, name, shape, dt=F32):
            return TB(stack.enter_context(nc.sbuf_tensor(un(name), list(shape), dt)), name)

        def sbn(stack, name, shape, dt, n):
            return Rot([sb(stack, "%s%d" % (name, i), shape, dt) for i in range(n)])

        def ps(stack, name, shape=(128, 512), dt=F32):
            return TB(stack.enter_context(nc.psum_tensor(un(name), list(shape), dt)), name)

        def psn(stack, name, n, shape=(128, 512)):
            return Rot([ps(stack, "%s%d" % (name, i), shape) for i in range(n)])

        cm = sb(top, "cm", [128, 3, 128])
        ones = sb(top, "ones", [128, 128])
        epsb = sb(top, "epsb", [128, 1])
        bonesb = sb(top, "bonesb", [128, 128], BF16)
        epsx = sb(top, "epsx", [128, 1])
        vecs = sb(top, "vecs", [128, cfg.NVEC])
        sc2 = sb(top, "sc2", [128, 8, 2])
        MS = []
        for par in range(2):
            MS.append((sb(top, "mod", [128, 24, 2]), sb(top, "gmod", [128, 8, 2]),
                       [sb(top, "gtb%d" % v, [128, D]) for v in range(2)],
                       sb(top, "gk8", [128, 1]), sb(top, "scd", [128, 2 * NLC])))
        wg32 = nc_wg32 = top.enter_context(nc.sbuf_tensor(un("wg32"), [128, 4, 128], F32))
        wgB = [[Buf("wg%d%d" % (h, dg)) for dg in range(4)] for h in range(2)]

        ident = cm.t[:, 0, :]
        bones = cm.t[:, 1, :]
        rmat = cm.t[:, 2, :]

        S.dma(lambda e: e.dma_start(out=cm.t[:], in_=cmats_d.rearrange("c p n -> p c n")), writes=[cm.b])
        S.dma(lambda e: e.dma_start(out=vecs.t[:], in_=vecs_d), writes=[vecs.b])
        S.op("pool", lambda e: e.memset(ones.t[:], 1.0), writes=[ones.b])
        S.op("pool", lambda e: e.memset(epsb.t[:], HD * EPS), writes=[epsb.b])
        S.op("pool", lambda e: e.memset(epsx.t[:], EPS), writes=[epsx.b])
        S.op("pool", lambda e: e.memset(wg32[:], 0.0), writes=[b for row in wgB for b in row])
        S.op("pool", lambda e: e.tensor_copy(out=bonesb.t[:], in_=cm.t[:, 1, :]), reads=[cm.b], writes=[bonesb.b])
        for v, col in ((0, cfg.V_C), (1, cfg.V_CX)):
            S.op("act", lambda e, v=v, col=col: e.activation(out=sc2.t[:, :, v], in_=vecs.t[:, col:col + 8], func=AF.Silu),
                 reads=[vecs.b], writes=[sc2.b])
        S.end_phase()

        def emit_M(lm, ph, pgt_slots=2):
            mod, gmod, gtb, gk8, scd = MS[lm % 2]
            vo_m = lm * cfg.VL

            def vcol(off, n=1):
                return vecs.t[:, vo_m + off:vo_m + off + n]

            wst = sbn(ph, "wstm", [128, 8, 512], F32, 2)
            pmod = ps(ph, "pmod", [128, 24, 2])
            pgt = psn(ph, "pgt", pgt_slots)
            grep = sbn(ph, "grep", [128, 128], F32, 2)
            for g in range(6):
                w = wst.next()
                S.dma(lambda e, w=w, g=g: e.dma_start(
                    out=w.t[:], in_=w_mod_d[lm, :, g * 512:(g + 1) * 512].rearrange("(kc p) n -> p kc n", p=128)),
                    writes=[w.b])
                for f4 in range(4):
                    fc = g * 4 + f4
                    for kc in range(8):
                        S.op("pe", lambda e, w=w, f4=f4, fc=fc, kc=kc: e.matmul(
                            pmod.t[:, fc, :], lhsT=w.t[:, kc, f4 * 128:(f4 + 1) * 128], rhs=sc2.t[:, kc, :],
                            start=(kc == 0), stop=(kc == 7)), reads=[w.b, sc2.b], writes=[pmod.b])
                yield
            for v in range(2):
                S.op("dve", lambda e, v=v: e.tensor_tensor(out=mod.t[:, :, v], in0=pmod.t[:, :, v],
                                                           in1=vcol(cfg.V_BM, 24), op=ALU.add),
                     reads=[pmod.b, vecs.b], writes=[mod.b])
            for v in range(2):
                S.op("dve", lambda e, v=v: e.scalar_tensor_tensor(
                    out=gmod.t[:, :, v], in0=mod.t[:, 8:16, v], scalar=1.0, in1=vcol(cfg.V_NG, 8),
                    op0=ALU.add, op1=ALU.mult), reads=[mod.b, vecs.b], writes=[gmod.b])
            for v in range(2):
                for kc in range(8):
                    gr = grep.next()
                    pg = pgt.next()
                    S.op("dve", lambda e, gr=gr, kc=kc, v=v: e.tensor_scalar(
                        out=gr.t[:], in0=ones.t[:], scalar1=mod.t[:, 16 + kc, v:v + 1], scalar2=None, op0=ALU.mult),
                        reads=[ones.b, mod.b], writes=[gr.b])
                    S.op("pe", lambda e, gr=gr, pg=pg: e.matmul(pg.t[:, 0:128], lhsT=gr.t[:], rhs=ident,
                                                               start=True, stop=True),
                         reads=[gr.b, cm.b], writes=[pg.b])
                    S.op("dve", lambda e, pg=pg, kc=kc, v=v: e.tensor_copy(
                        out=gtb[v].t[:, kc * 128:(kc + 1) * 128], in_=pg.t[:, 0:128]),
                        reads=[pg.b], writes=[gtb[v].b])
            S.op("dve", lambda e: e.tensor_scalar(out=gk8.t[:], in0=vcol(cfg.V_KG), scalar1=8.0, scalar2=None,
                                                  op0=ALU.mult), reads=[vecs.b], writes=[gk8.b])
            S.op("act", lambda e: e.activation(out=scd.t[:], in_=vcol(cfg.V_LM, 2 * NLC), func=AF.Exp, scale=-1.0),
                 reads=[vecs.b], writes=[scd.b])
            S.op("act", lambda e: e.activation(out=scd.t[:], in_=scd.t[:], func=AF.Ln, bias=1.0, scale=1.0),
                 reads=[scd.b], writes=[scd.b])
            S.op("dve", lambda e: e.tensor_scalar(out=scd.t[:], in0=scd.t[:], scalar1=-8.0, scalar2=None, op0=ALU.mult),
                 reads=[scd.b], writes=[scd.b])


        with ExitStack() as ph:
            for _ in emit_M(0, ph):
                pass
            S.end_phase()

        for l in range(L):
            mod, gmod, gtb, gk8, scd = MS[l % 2]
            need_ctx = l < cfg.total_depth - 1
            last = l == cfg.total_depth - 1
            xsrc = xc_d if l == 0 else X_s
            vo = l * cfg.VL

            def vcol(off, n=1, vo=vo):
                return vecs.t[:, vo + off:vo + off + n]

            with ExitStack() as ph:
                hT = ph.enter_context(nc.sbuf_tensor(un("hT"), [128, 8, T], BF16))
                hB = [Buf("hT%d" % i) for i in range(NBLK)]
                wst = sbn(ph, "wst", [128, 8, 512], F32, 2)
                wbf = sbn(ph, "wbf", [128, 8, 512], BF16, 2)
                pmm = psn(ph, "pmm", 3)
                with ExitStack() as ph1:
                    xt = sbn(ph1, "xt", [128, D], F32, 3)
                    xs = sbn(ph1, "xs", [128, D], F32, 2)
                    junk = sbn(ph1, "junk", [128, D], BF16, 2)
                    ss = sbn(ph1, "ss", [128, 1], F32, 3)
                    rs = sbn(ph1, "rs", [128, 1], F32, 3)
                    ptr = psn(ph1, "ptr", 2)
                    genM = emit_M(l + 1, ph1, pgt_slots=1) if (l + 1 < L and cfg.split == 2) else None
                    for tb in range(NBLK):
                        if genM is not None and tb % 5 == 2:
                            next(genM, None)
                        v = 1 if tb < NCTX // 128 else 0
                        x_, xs_, jk, ss_, rs_ = xt.next(), xs.next(), junk.next(), ss.next(), rs.next()
                        S.dma(lambda e, x_=x_, tb=tb: e.dma_start(out=x_.t[:], in_=xsrc[tb * 128:(tb + 1) * 128, :]),
                              writes=[x_.b])
                        S.op("act", lambda e, x_=x_, jk=jk, ss_=ss_: e.activation(
                            out=jk.t[:], in_=x_.t[:], func=AF.Square, accum_out=ss_.t[:, 0:1]),
                            reads=[x_.b], writes=[jk.b, ss_.b])
                        S.op("act", lambda e, ss_=ss_, rs_=rs_: e.activation(
                            out=rs_.t[:], in_=ss_.t[:], func=AF.Sqrt, bias=epsx.t[:, 0:1], scale=1.0 / D),
                            reads=[ss_.b, epsx.b], writes=[rs_.b])
                        S.op("dve", lambda e, rs_=rs_: e.reciprocal(out=rs_.t[:], in_=rs_.t[:]),
                            reads=[rs_.b], writes=[rs_.b])
                        S.op("dve", lambda e, x_=x_, xs_=xs_, rs_=rs_: e.tensor_scalar(
                            out=xs_.t[:], in0=x_.t[:], scalar1=rs_.t[:, 0:1], scalar2=None, op0=ALU.mult),
                            reads=[x_.b, rs_.b], writes=[xs_.b])
                        for half in range(2):
                            p_ = ptr.next()
                            for q4 in range(4):
                                kc = half * 4 + q4
                                S.op("pe", lambda e, p_=p_, xs_=xs_, q4=q4, kc=kc: e.transpose(
                                    out=p_.t[:, q4 * 128:(q4 + 1) * 128], in_=xs_.t[:, kc * 128:(kc + 1) * 128], identity=ident),
                                    reads=[xs_.b, cm.b], writes=[p_.b])
                            for q4 in range(4):
                                kc = half * 4 + q4
                                S.op("act", lambda e, p_=p_, q4=q4, kc=kc, tb=tb, v=v: e.activation(
                                    out=hT[:, kc, tb * 128:(tb + 1) * 128], in_=p_.t[:, q4 * 128:(q4 + 1) * 128],
                                    func=AF.Identity, bias=mod.t[:, kc, v:v + 1], scale=gmod.t[:, kc, v:v + 1]),
                                    reads=[p_.b, mod.b, gmod.b], writes=[hB[tb]])
                    if genM is not None:
                        for _ in genM:
                            pass
                with ExitStack() as ph2:
                    sq = sbn(ph2, "sq", [128, 512], BF16, 3)
                    rstd = sbn(ph2, "rstd", [128, 512], F32, 2)
                    qn = sbn(ph2, "qn", [128, 512], F32, 3)
                    t1 = sbn(ph2, "t1", [128, 512], F32, 2)
                    t2 = sbn(ph2, "t2", [128, 512], F32, 2)
                    qr = sbn(ph2, "qr", [128, 512], BF16, 3)
                    o16 = sbn(ph2, "o16", [128, 512], BF16, 3)
                    o32 = sbn(ph2, "o32", [128, 512], F32, 2)
                    rC = sbn(ph2, "rC", [128, 512], F32, 3)
                    rS = sbn(ph2, "rS", [128, 512], F32, 3)
                    vt = sbn(ph2, "vt", [128, NV], BF16, 2)
                    pss = psn(ph2, "pss", 2)
                    prot = psn(ph2, "prot", 2)

                    def qk_stage0(u):
                        pm = u["pm"] = pmm.next()
                        wb_, c4, t0, n = u["wb"], u["c4"], u["t0"], u["n"]
                        hb = [hB[i] for i in range(t0 // 128, (t0 + n) // 128)]
                        for kc in range(8):
                            S.op("pe", lambda e, pm=pm, wb_=wb_, kc=kc, c4=c4, t0=t0, n=n: e.matmul(
                                pm.t[:, 0:n], lhsT=wb_.t[:, kc, c4 * 128:(c4 + 1) * 128], rhs=hT[:, kc, t0:t0 + n],
                                start=(kc == 0), stop=(kc == 7)), reads=hb + [wb_.b], writes=[pm.b])
                        sq_ = u["sq"] = sq.next()
                        S.op("act", lambda e, sq_=sq_, pm=pm, n=n: e.activation(
                            out=sq_.t[:, 0:n], in_=pm.t[:, 0:n], func=AF.Square), reads=[pm.b], writes=[sq_.b])
                        if not u["isctx"]:
                            rC_, rS_ = u["rC"], u["rS"] = rC.next(), rS.next()
                            l0 = t0 - NCTX
                            S.dma(lambda e, rC_=rC_, l0=l0, n=n: e.dma_start(out=rC_.t[:, 0:n], in_=rope_d[0, :, l0:l0 + n]),
                                  writes=[rC_.b])
                            S.dma(lambda e, rS_=rS_, l0=l0, n=n: e.dma_start(out=rS_.t[:, 0:n], in_=rope_d[1, :, l0:l0 + n]),
                                  writes=[rS_.b])

                    def qk_stage1(u):
                        pm, sq_, n, typ = u["pm"], u["sq"], u["n"], u["typ"]
                        gvec = vcol(cfg.V_QG) if typ == "q" else gk8.t[:, 0:1]
                        gbuf = vecs.b if typ == "q" else gk8.b
                        pss_, rstd_ = pss.next(), rstd.next()
                        S.op("pe", lambda e, pss_=pss_, sq_=sq_, n=n: e.matmul(
                            pss_.t[:, 0:n], lhsT=bonesb.t[:], rhs=sq_.t[:, 0:n], start=True, stop=True),
                            reads=[sq_.b, bonesb.b], writes=[pss_.b])
                        S.op("act", lambda e, rstd_=rstd_, pss_=pss_, n=n: e.activation(
                            out=rstd_.t[:, 0:n], in_=pss_.t[:, 0:n], func=AF.Ln, bias=epsb.t[:, 0:1], scale=1.0),
                            reads=[pss_.b, epsb.b], writes=[rstd_.b])
                        S.op("act", lambda e, rstd_=rstd_, n=n: e.activation(
                            out=rstd_.t[:, 0:n], in_=rstd_.t[:, 0:n], func=AF.Exp, scale=-0.5),
                            reads=[rstd_.b], writes=[rstd_.b])
                        if u["isctx"]:
                            o_ = u["q"] = qr.next()
                        else:
                            o_ = u["qn"] = qn.next()
                        S.op("dve", lambda e, o_=o_, pm=pm, rstd_=rstd_, n=n, gvec=gvec: e.scalar_tensor_tensor(
                            out=o_.t[:, 0:n], in0=pm.t[:, 0:n], scalar=gvec, in1=rstd_.t[:, 0:n],
                            op0=ALU.mult, op1=ALU.mult), reads=[pm.b, rstd_.b, gbuf], writes=[o_.b])

                    def qk_stage2(u):
                        n, t0, typ, idx = u["n"], u["t0"], u["typ"], u["idx"]
                        dst = QT_s if typ == "q" else KT_s
                        if not u["isctx"]:
                            qn_, rC_, rS_ = u["qn"], u["rC"], u["rS"]
                            pr_ = prot.next()
                            S.op("pe", lambda e, pr_=pr_, qn_=qn_, n=n: e.matmul(
                                pr_.t[:, 0:n], lhsT=rmat, rhs=qn_.t[:, 0:n], start=True, stop=True),
                                reads=[qn_.b, cm.b], writes=[pr_.b])
                            t1_, t2_, q_ = t1.next(), t2.next(), qr.next()
                            S.op("pool", lambda e, t1_=t1_, qn_=qn_, rC_=rC_, n=n: e.tensor_tensor(
                                out=t1_.t[:, 0:n], in0=qn_.t[:, 0:n], in1=rC_.t[:, 0:n], op=ALU.mult),
                                reads=[qn_.b, rC_.b], writes=[t1_.b])
                            S.op("dve", lambda e, t2_=t2_, pr_=pr_, rS_=rS_, n=n: e.tensor_tensor(
                                out=t2_.t[:, 0:n], in0=pr_.t[:, 0:n], in1=rS_.t[:, 0:n], op=ALU.mult),
                                reads=[pr_.b, rS_.b], writes=[t2_.b])
                            S.op("pool", lambda e, q_=q_, t1_=t1_, t2_=t2_, n=n: e.tensor_tensor(
                                out=q_.t[:, 0:n], in0=t1_.t[:, 0:n], in1=t2_.t[:, 0:n], op=ALU.add),
                                reads=[t1_.b, t2_.b], writes=[q_.b])
                        else:
                            q_ = u["q"]
                        S.dma(lambda e, q_=q_, dst=dst, idx=idx, t0=t0, n=n: e.dma_start(
                            out=dst[idx, :, t0:t0 + n], in_=q_.t[:, 0:n]), reads=[q_.b], Q="pool")

                    fl = [None, None]

                    def qk_step(u):
                        if u is not None:
                            qk_stage0(u)
                        if fl[0] is not None:
                            qk_stage1(fl[0])
                        if fl[1] is not None:
                            qk_stage2(fl[1])
                        fl[1] = fl[0]
                        fl[0] = u

                    def qk_flush():
                        if fl[0] is not None or fl[1] is not None:
                            qk_step(None)
                            qk_step(None)

                    ngroups = (NCOLS + 511) // 512
                    for g in range(ngroups):
                        c0 = g * 512
                        ncol = min(512, NCOLS - c0)
                        w, wb_ = wst.next(), wbf.next()
                        S.dma(lambda e, w=w, c0=c0, ncol=ncol: e.dma_start(
                            out=w.t[:, :, 0:ncol], in_=w_in_d[l, :, c0:c0 + ncol].rearrange("(kc p) n -> p kc n", p=128)),
                            writes=[w.b])
                        S.op("pool", lambda e, w=w, wb_=wb_, ncol=ncol: e.tensor_copy(out=wb_.t[:, :, 0:ncol], in_=w.t[:, :, 0:ncol]),
                             reads=[w.b], writes=[wb_.b])
                        done_v = False
                        for c4 in range(ncol // 128):
                            typ, idx = cfg.chunks[g * 4 + c4]
                            if typ == "v":
                                if done_v:
                                    continue
                                done_v = True
                                assert c4 * 128 + NV <= ncol
                                for tb in range(NBLK):
                                    pv = pmm.next()
                                    for kc in range(8):
                                        S.op("pe", lambda e, pv=pv, wb_=wb_, kc=kc, tb=tb, c4=c4: e.matmul(
                                            pv.t[:, 0:NV], lhsT=hT[:, kc, tb * 128:(tb + 1) * 128],
                                            rhs=wb_.t[:, kc, c4 * 128:c4 * 128 + NV], start=(kc == 0), stop=(kc == 7)),
                                            reads=[hB[tb], wb_.b], writes=[pv.b])
                                    v_ = vt.next()
                                    S.op("act", lambda e, pv=pv, v_=v_: e.activation(out=v_.t[:], in_=pv.t[:, 0:NV], func=AF.Copy),
                                         reads=[pv.b], writes=[v_.b])
                                    S.dma(lambda e, v_=v_, tb=tb: e.dma_start(out=V_s[tb * 128:(tb + 1) * 128, :], in_=v_.t[:]),
                                          reads=[v_.b], Q="act")
                                continue
                            if typ in ("q", "k"):
                                for (t0, n, isctx) in tiles:
                                    qk_step({"typ": typ, "idx": idx, "t0": t0, "n": n, "isctx": isctx, "wb": wb_, "c4": c4})
                                continue
                            qk_flush()
                            for (t0, n, isctx) in tiles:
                                pm = pmm.next()
                                hb = [hB[i] for i in range(t0 // 128, (t0 + n) // 128)]
                                for kc in range(8):
                                    S.op("pe", lambda e, pm=pm, wb_=wb_, kc=kc, c4=c4, t0=t0, n=n: e.matmul(
                                        pm.t[:, 0:n], lhsT=wb_.t[:, kc, c4 * 128:(c4 + 1) * 128], rhs=hT[:, kc, t0:t0 + n],
                                        start=(kc == 0), stop=(kc == 7)), reads=hb + [wb_.b], writes=[pm.b])
                                if typ in ("ga", "gb", "gm"):
                                    o_ = o16.next()
                                    fn = AF.Sigmoid if typ == "gm" else AF.Silu
                                    dst = {"ga": SGA_s, "gb": SGB_s, "gm": SGM_s}[typ]
                                    S.op("act", lambda e, o_=o_, pm=pm, n=n, fn=fn: e.activation(
                                        out=o_.t[:, 0:n], in_=pm.t[:, 0:n], func=fn), reads=[pm.b], writes=[o_.b])
                                    S.dma(lambda e, o_=o_, dst=dst, idx=idx, t0=t0, n=n: e.dma_start(
                                        out=dst[idx, :, t0:t0 + n], in_=o_.t[:, 0:n]), reads=[o_.b], Q="act")
                                else:
                                    assert typ == "u"
                                    o_ = o32.next()
                                    S.op("act", lambda e, o_=o_, pm=pm, n=n: e.activation(
                                        out=o_.t[:, 0:n], in_=pm.t[:, 0:n], func=AF.Copy), reads=[pm.b], writes=[o_.b])
                                    S.dma(lambda e, o_=o_, idx=idx, t0=t0, n=n: e.dma_start(
                                        out=UT_s[idx, :, t0:t0 + n], in_=o_.t[:, 0:n]), reads=[o_.b], Q="act")
                    qk_flush()
                S.end_phase()

            with ExitStack() as ph:
                u_ = sb(ph, "u", [128, T])
                uc = sb(ph, "uc", [128, T])
                ucb = sb(ph, "ucb", [128, T], BF16)
                ad = [sb(ph, "a%d" % d, [128, T]) for d in range(2)]
                bd = [sb(ph, "bb%d" % d, [128, T]) for d in range(2)]
                hd = [sb(ph, "hd%d" % d, [128, T]) for d in range(2)]
                sgb = sb(ph, "sgb", [128, T], BF16)
                hg = sb(ph, "hg", [128, T], BF16)
                wgbf = sb(ph, "wgbf", [128, 4, 128], BF16)
                tt_ = sbn(ph, "tt", [128, 512], F32, 2)
                mm_ = sbn(ph, "mm", [128, 512], F32, 2)
                bi = sbn(ph, "bi", [128, 512], F32, 2)
                pr = psn(ph, "pr", 2)
                pi = psn(ph, "pi", 2)
                aBd = [[Buf("aB%d" % i) for i in range(len(tiles))] for d in range(2)]
                bBd = [[Buf("bB%d" % i) for i in range(len(tiles))] for d in range(2)]
                pstep = ad[0].t[:].ap[0][0]

                def revap(tns, t_last, n):
                    return bass.AP(tns, t_last, [[pstep, 128], [-1, n]])

                for cc in range(NLC):
                    S.dma(lambda e, cc=cc: e.dma_start(out=u_.t[:], in_=UT_s[cc]), writes=[u_.b])
                    S.dma(lambda e, cc=cc: e.dma_start(out=sgb.t[:], in_=SGB_s[cc]), writes=[sgb.b])
                    for d in range(2):
                        for g in range(2):
                            for h in range(2):
                                S.dma(lambda e, d=d, g=g, h=h, cc=cc: e.dma_start(
                                    out=wg32[h * 64:(h + 1) * 64, d * 2 + g, h * 64:(h + 1) * 64],
                                    in_=gate_w_d[l, d, g, 2 * cc + h]), writes=[wgB[h][d * 2 + g]])
                    S.op("act", lambda e: e.activation(out=wgbf.t[:], in_=wg32[:], func=AF.Copy),
                         reads=[b for row in wgB for b in row], writes=[wgbf.b])
                    cw = lambda j, cc=cc: vcol(cfg.V_CW + j * NLC + cc)
                    S.op("act", lambda e, cc=cc, cw=cw: e.activation(
                        out=uc.t[:], in_=u_.t[:], func=AF.Identity, bias=vcol(cfg.V_CB + cc), scale=cw(2)),
                        reads=[u_.b, vecs.b], writes=[uc.b])
                    for (s0, e0) in ((0, NCTX), (NCTX, T)):
                        for (j, do, di, ln) in ((0, 2, 0, e0 - s0 - 2), (1, 1, 0, e0 - s0 - 1), (3, 0, 1, e0 - s0 - 1)):
                            S.op("dve", lambda e, s0=s0, j=j, do=do, di=di, ln=ln, cw=cw: e.scalar_tensor_tensor(
                                out=uc.t[:, s0 + do:s0 + do + ln], in0=u_.t[:, s0 + di:s0 + di + ln], scalar=cw(j),
                                in1=uc.t[:, s0 + do:s0 + do + ln], op0=ALU.mult, op1=ALU.add),
                                reads=[u_.b, uc.b, vecs.b], writes=[uc.b])
                    S.op("act", lambda e: e.activation(out=ucb.t[:], in_=uc.t[:], func=AF.Copy), reads=[uc.b], writes=[ucb.b])
                    for d in range(2):
                        cr = cfg.V_GB + (d * 2 + 0) * NLC + cc
                        ci = cfg.V_GB + (d * 2 + 1) * NLC + cc
                        a_, bb, aB, bB = ad[d], bd[d], aBd[d], bBd[d]
                        for ti, (t0, n, isctx) in enumerate(tiles):
                            pr_, pi_ = pr.next(), pi.next()
                            S.op("pe", lambda e, pr_=pr_, d=d, t0=t0, n=n: e.matmul(
                                pr_.t[:, 0:n], lhsT=wgbf.t[:, d * 2 + 0, :], rhs=ucb.t[:, t0:t0 + n], start=True, stop=True),
                                reads=[wgbf.b, ucb.b], writes=[pr_.b])
                            S.op("pe", lambda e, pi_=pi_, d=d, t0=t0, n=n: e.matmul(
                                pi_.t[:, 0:n], lhsT=wgbf.t[:, d * 2 + 1, :], rhs=ucb.t[:, t0:t0 + n], start=True, stop=True),
                                reads=[wgbf.b, ucb.b], writes=[pi_.b])
                            S.op("act", lambda e, pr_=pr_, t0=t0, n=n, cr=cr, a_=a_: e.activation(
                                out=a_.t[:, t0:t0 + n], in_=pr_.t[:, 0:n], func=AF.Sigmoid, bias=vcol(cr), scale=1.0),
                                reads=[pr_.b, vecs.b], writes=[aB[ti]])
                            S.op("act", lambda e, pi_=pi_, t0=t0, n=n, ci=ci, bb=bb: e.activation(
                                out=bb.t[:, t0:t0 + n], in_=pi_.t[:, 0:n], func=AF.Sigmoid, bias=vcol(ci), scale=1.0),
                                reads=[pi_.b, vecs.b], writes=[bB[ti]])
                    for d in range(2):
                        sidx = d * NLC + cc
                        a_, aB = ad[d], aBd[d]
                        for ti, (t0, n, isctx) in enumerate(tiles):
                            S.op("act", lambda e, t0=t0, n=n, sidx=sidx, a_=a_: e.activation(
                                out=a_.t[:, t0:t0 + n], in_=a_.t[:, t0:t0 + n], func=AF.Exp, scale=scd.t[:, sidx:sidx + 1]),
                                reads=[aB[ti], scd.b], writes=[aB[ti]])
                    for d in range(2):
                        a_, bb, aB, bB = ad[d], bd[d], aBd[d], bBd[d]
                        for ti, (t0, n, isctx) in enumerate(tiles):
                            t_, m_, bi_ = tt_.next(), mm_.next(), bi.next()
                            S.op("act", lambda e, t_=t_, t0=t0, n=n, a_=a_: e.activation(
                                out=t_.t[:, 0:n], in_=a_.t[:, t0:t0 + n], func=AF.Square),
                                reads=[aB[ti]], writes=[t_.b])
                            S.op("act", lambda e, m_=m_, t_=t_, n=n: e.activation(
                                out=m_.t[:, 0:n], in_=t_.t[:, 0:n], func=AF.Sqrt, bias=1.0, scale=-1.0),
                                reads=[t_.b], writes=[m_.b])
                            S.op("dve", lambda e, bi_=bi_, t0=t0, n=n, bb=bb: e.tensor_tensor(
                                out=bi_.t[:, 0:n], in0=bb.t[:, t0:t0 + n], in1=uc.t[:, t0:t0 + n], op=ALU.mult),
                                reads=[bB[ti], uc.b], writes=[bi_.b])
                            S.op("pool", lambda e, bi_=bi_, m_=m_, t0=t0, n=n, bb=bb: e.tensor_tensor(
                                out=bb.t[:, t0:t0 + n], in0=bi_.t[:, 0:n], in1=m_.t[:, 0:n], op=ALU.mult),
                                reads=[bi_.b, m_.b], writes=[bB[ti]])
                        h_ = hd[d]
                        if d == 0:
                            S.op("dve", lambda e, h_=h_, a_=a_, bb=bb: e.tensor_tensor_scan(
                                out=h_.t[:], data0=a_.t[:], data1=bb.t[:], initial=0.0, op0=ALU.mult, op1=ALU.add),
                                reads=aB + bB, writes=[h_.b])
                        else:
                            S.op("dve", lambda e, h_=h_, a_=a_, bb=bb: e.tensor_tensor_scan(
                                out=revap(h_.t, NCTX - 1, NCTX), data0=revap(a_.t, NCTX - 1, NCTX),
                                data1=revap(bb.t, NCTX - 1, NCTX), initial=0.0, op0=ALU.mult, op1=ALU.add),
                                reads=aB + bB, writes=[h_.b])
                            S.op("dve", lambda e, h_=h_, a_=a_, bb=bb: e.tensor_tensor_scan(
                                out=revap(h_.t, T - 1, NLAT), data0=revap(a_.t, T - 1, NLAT),
                                data1=revap(bb.t, T - 1, NLAT), initial=h_.t[:, 0:1], op0=ALU.mult, op1=ALU.add),
                                reads=aB + bB + [h_.b], writes=[h_.b])
                    S.op("dve", lambda e: e.tensor_tensor(out=hd[0].t[:], in0=hd[0].t[:], in1=hd[1].t[:], op=ALU.add),
                         reads=[hd[0].b, hd[1].b], writes=[hd[0].b])
                    S.op("dve", lambda e: e.tensor_tensor(out=hg.t[:], in0=hd[0].t[:], in1=sgb.t[:], op=ALU.mult),
                         reads=[hd[0].b, sgb.b], writes=[hg.b])
                    S.dma(lambda e, cc=cc: e.dma_start(out=HG_s[cc], in_=hg.t[:]), reads=[hg.b], Q="pool")
                S.end_phase()

            with ExitStack() as ph:
                VW = NKC * 256 + 64
                kt = sb(ph, "kt", [128, NKC, T], BF16)
                vp = ph.enter_context(nc.sbuf_tensor(un("vp"), [128, NBLK, VW], BF16))
                vpB = Buf("vp")
                qT = sbn(ph, "qT", [128, 2, 512], BF16, 2)
                for q_ in qT.items:
                    S.op("pool", lambda e, q_=q_: e.memset(q_.t[:], 0.0), writes=[q_.b])
                sga = sbn(ph, "sga", [128, 512], BF16, 2)
                P = sbn(ph, "P", [128, 1024], BF16, 4)
                rd = sbn(ph, "rd", [128, 512], F32, 2)
                og32 = sbn(ph, "og32", [128, 512], F32, 2)
                ogt = sbn(ph, "ogt", [128, 512], BF16, 2)
                ocp = [sbn(ph, "ocp%d" % hh, [128, 512], F32, 2) for hh in range(2)]
                Sps = psn(ph, "S", 3, shape=(128, 1024))
                Ops = [ps(ph, "O0"), ps(ph, "O1")]
                LOOK = 2
                S.dma(lambda e: e.dma_start(out=kt.t[:], in_=KT_s.rearrange("c p t -> p c t")), writes=[kt.b])
                S.op("pool", lambda e: e.memset(vp[:], 1.0), writes=[vpB])
                for gi in range(2 * NKC):
                    S.dma(lambda e, gi=gi: e.dma_start(
                        out=vp[:, :, gi * 128 + 64:gi * 128 + 128],
                        in_=V_s[:, gi * 64:(gi + 1) * 64].rearrange("(kb p) d -> p kb d", p=128)), writes=[vpB])
                qtiles = [tl for tl in tiles if (not tl[2]) or need_ctx]
                for j in range(NQC):
                    c = j // 4 if cfg.split == 1 else 0
                    for (t0, n, isctx) in qtiles:
                        nkb = NCTX // 128 if isctx else NBLK
                        q_, sg_ = qT.next(), sga.next()
                        S.dma(lambda e, q_=q_, j=j, t0=t0, n=n: e.dma_start(out=q_.t[0:64, 0, 0:n], in_=QT_s[j, 0:64, t0:t0 + n]), writes=[q_.b])
                        S.dma(lambda e, q_=q_, j=j, t0=t0, n=n: e.dma_start(out=q_.t[64:128, 1, 0:n], in_=QT_s[j, 64:128, t0:t0 + n]), writes=[q_.b])
                        S.dma(lambda e, sg_=sg_, j=j, t0=t0, n=n: e.dma_start(out=sg_.t[:, 0:n], in_=SGA_s[j, :, t0:t0 + n]), writes=[sg_.b])
                        O = Ops

                        def emitS(kb, q_=q_, n=n, c=c):
                            s_ = Sps.next()
                            p_ = P.next()
                            for hh in range(2):
                                S.op("pe", lambda e, s_=s_, hh=hh, kb=kb: e.matmul(
                                    s_.t[:, hh * 512:hh * 512 + n], lhsT=kt.t[:, c, kb * 128:(kb + 1) * 128],
                                    rhs=q_.t[:, hh, 0:n], start=True, stop=True),
                                    reads=[kt.b, q_.b], writes=[s_.b])
                            if n == 512:
                                S.op("act", lambda e, s_=s_, p_=p_: e.activation(out=p_.t[:, :], in_=s_.t[:, :], func=AF.Exp),
                                     reads=[s_.b], writes=[p_.b])
                            else:
                                for hh in range(2):
                                    S.op("act", lambda e, s_=s_, p_=p_, hh=hh: e.activation(
                                        out=p_.t[:, hh * 512:hh * 512 + n], in_=s_.t[:, hh * 512:hh * 512 + n], func=AF.Exp),
                                        reads=[s_.b], writes=[p_.b])
                            return (s_, p_)

                        def emitPV(kb, sp, n=n, c=c, O=O, nkb=nkb):
                            for hh in range(2):
                                gi = 2 * c + hh
                                col0 = gi * 128 + 64 if hh == 0 else gi * 128
                                p_ = sp[1]
                                S.op("pe", lambda e, hh=hh, p_=p_, col0=col0, kb=kb: e.matmul(
                                    O[hh].t[:, 0:n], lhsT=vp[:, kb, col0:col0 + 128], rhs=p_.t[:, hh * 512:hh * 512 + n],
                                    start=(kb == 0), stop=(kb == nkb - 1)), reads=[vpB, p_.b], writes=[O[hh].b])

                        queue = [emitS(k) for k in range(min(LOOK, nkb))]
                        for kb in range(nkb):
                            if kb + LOOK < nkb:
                                queue.append(emitS(kb + LOOK))
                            emitPV(kb, queue.pop(0))
                        rd_, og_, ot_ = rd.next(), og32.next(), ogt.next()
                        oc = [ocp[0].next(), ocp[1].next()]
                        for hh in range(2):
                            S.op("dve", lambda e, hh=hh, oc=oc, O=O, n=n: e.tensor_copy(out=oc[hh].t[:, 0:n], in_=O[hh].t[:, 0:n]),
                                 reads=[O[hh].b], writes=[oc[hh].b])
                        S.op("dve", lambda e, rd_=rd_, oc=oc, n=n: e.reciprocal(out=rd_.t[0:64, 0:n], in_=oc[0].t[64:128, 0:n]),
                             reads=[oc[0].b], writes=[rd_.b])
                        S.op("dve", lambda e, rd_=rd_, oc=oc, n=n: e.reciprocal(out=rd_.t[64:128, 0:n], in_=oc[1].t[0:64, 0:n]),
                             reads=[oc[1].b], writes=[rd_.b])
                        S.op("pool", lambda e, rd_=rd_, og_=og_, oc=oc, n=n: e.tensor_tensor(
                            out=og_.t[0:64, 0:n], in0=oc[0].t[0:64, 0:n], in1=rd_.t[0:64, 0:n], op=ALU.mult),
                            reads=[oc[0].b, rd_.b], writes=[og_.b])
                        S.op("pool", lambda e, rd_=rd_, og_=og_, oc=oc, n=n: e.tensor_tensor(
                            out=og_.t[64:128, 0:n], in0=oc[1].t[64:128, 0:n], in1=rd_.t[64:128, 0:n], op=ALU.mult),
                            reads=[oc[1].b, rd_.b], writes=[og_.b])
                        S.op("pool", lambda e, og_=og_, ot_=ot_, sg_=sg_, n=n: e.tensor_tensor(
                            out=ot_.t[:, 0:n], in0=og_.t[:, 0:n], in1=sg_.t[:, 0:n], op=ALU.mult),
                            reads=[og_.b, sg_.b], writes=[ot_.b])
                        S.dma(lambda e, ot_=ot_, j=j, t0=t0, n=n: e.dma_start(out=OG_s[j, :, t0:t0 + n], in_=ot_.t[:, 0:n]),
                              reads=[ot_.b], Q="pool")
                S.end_phase()

            with ExitStack() as ph:
                wst = sbn(ph, "wstd", [128, 8, 512], F32, 2)
                wa = sb(ph, "wa", [128, NQC, D], BF16)
                wb = sb(ph, "wb", [128, NLC, D], BF16)
                wo = sb(ph, "wo", [128, 8, D], BF16)
                og = sbn(ph, "og", [128, NQC, 512], BF16, 2)
                hgt = sbn(ph, "hgt", [128, NLC, 512], BF16, 2)
                sgm = sbn(ph, "sgm", [128, 16, 512], BF16, 2)
                mT = sb(ph, "mT", [128, 8, 512], BF16)
                mB = [Buf("mT%d" % i) for i in range(8)]
                y1 = sbn(ph, "y1", [128, 512], F32, 2)
                y2 = sbn(ph, "y2", [128, 512], F32, 2)
                xt = sbn(ph, "xtd", [128, D], F32, 2)
                xn = sbn(ph, "xn", [128, D], F32, 2)
                zz = sbn(ph, "zz", [128, 512], F32, 2)
                pya = psn(ph, "pya", 2)
                pyb = psn(ph, "pyb", 2)
                pz = psn(ph, "pz", 2)
                for (wdst, wsrc, nk) in ((wa, w_a_d, NQC), (wb, w_b_d, NLC), (wo, w_o_d, 8)):
                    for hf in range(2):
                        w = wst.next()
                        S.dma(lambda e, w=w, wsrc=wsrc, hf=hf, nk=nk: e.dma_start(
                            out=w.t[:, 0:nk, :], in_=wsrc[l, :, hf * 512:(hf + 1) * 512].rearrange("(kc p) n -> p kc n", p=128)),
                            writes=[w.b])
                        S.op("act", lambda e, w=w, wdst=wdst, hf=hf, nk=nk: e.activation(
                            out=wdst.t[:, :, hf * 512:(hf + 1) * 512], in_=w.t[:, 0:nk, :], func=AF.Copy), reads=[w.b], writes=[wdst.b])
                dtiles = [tl for tl in tiles if (not tl[2]) or need_ctx]
                for (t0, n, isctx) in dtiles:
                    v = 1 if isctx else 0
                    og_, hg_, sg_ = og.next(), hgt.next(), sgm.next()
                    S.dma(lambda e, og_=og_, t0=t0, n=n: e.dma_start(out=og_.t[:, :, 0:n], in_=OG_s[:, :, t0:t0 + n].rearrange("c p t -> p c t")),
                          writes=[og_.b])
                    S.dma(lambda e, hg_=hg_, t0=t0, n=n: e.dma_start(out=hg_.t[:, :, 0:n], in_=HG_s[:, :, t0:t0 + n].rearrange("c p t -> p c t")),
                          writes=[hg_.b])
                    S.dma(lambda e, sg_=sg_, t0=t0, n=n: e.dma_start(out=sg_.t[:, :, 0:n], in_=SGM_s[:, :, t0:t0 + n].rearrange("c p t -> p c t")),
                          writes=[sg_.b])
                    for fo in range(8):
                        pa, pb = pya.next(), pyb.next()
                        for j in range(NQC):
                            S.op("pe", lambda e, pa=pa, og_=og_, j=j, fo=fo, n=n: e.matmul(
                                pa.t[:, 0:n], lhsT=wa.t[:, j, fo * 128:(fo + 1) * 128], rhs=og_.t[:, j, 0:n],
                                start=(j == 0), stop=(j == NQC - 1)), reads=[wa.b, og_.b], writes=[pa.b])
                        for j in range(NLC):
                            S.op("pe", lambda e, pb=pb, hg_=hg_, j=j, fo=fo, n=n: e.matmul(
                                pb.t[:, 0:n], lhsT=wb.t[:, j, fo * 128:(fo + 1) * 128], rhs=hg_.t[:, j, 0:n],
                                start=(j == 0), stop=(j == NLC - 1)), reads=[wb.b, hg_.b], writes=[pb.b])
                        y1_, y2_ = y1.next(), y2.next()
                        S.op("dve", lambda e, y1_=y1_, pa=pa, sg_=sg_, fo=fo, n=n: e.tensor_tensor(
                            out=y1_.t[:, 0:n], in0=pa.t[:, 0:n], in1=sg_.t[:, fo, 0:n], op=ALU.mult),
                            reads=[pa.b, sg_.b], writes=[y1_.b])
                        S.op("dve", lambda e, y2_=y2_, pb=pb, sg_=sg_, fo=fo, n=n: e.tensor_tensor(
                            out=y2_.t[:, 0:n], in0=pb.t[:, 0:n], in1=sg_.t[:, 8 + fo, 0:n], op=ALU.mult),
                            reads=[pb.b, sg_.b], writes=[y2_.b])
                        S.op("pool", lambda e, y1_=y1_, y2_=y2_, fo=fo, n=n: e.tensor_tensor(
                            out=mT.t[:, fo, 0:n], in0=y1_.t[:, 0:n], in1=y2_.t[:, 0:n], op=ALU.add),
                            reads=[y1_.b, y2_.b], writes=[mB[fo]])
                    for tb in range(n // 128):
                        r0 = t0 + tb * 128
                        if cfg.split == 1:
                            x_, xn_ = xt.next(), xn.next()
                            S.dma(lambda e, x_=x_, r0=r0: e.dma_start(out=x_.t[:], in_=xsrc[r0:r0 + 128, :]), writes=[x_.b])
                        zb = []
                        for fh in range(2):
                            pz_ = pz.next()
                            for k in range(8):
                                S.op("pe", lambda e, pz_=pz_, k=k, tb=tb, fh=fh: e.matmul(
                                    pz_.t[:], lhsT=mT.t[:, k, tb * 128:(tb + 1) * 128], rhs=wo.t[:, k, fh * 512:(fh + 1) * 512],
                                    start=(k == 0), stop=(k == 7)), reads=[mB[k], wo.b], writes=[pz_.b])
                            z_ = zz.next()
                            if cfg.split == 1:
                                S.op("dve", lambda e, z_=z_, pz_=pz_, fh=fh, v=v: e.tensor_tensor(
                                    out=z_.t[:], in0=pz_.t[:], in1=gtb[v].t[:, fh * 512:(fh + 1) * 512], op=ALU.mult),
                                    reads=[pz_.b, gtb[v].b], writes=[z_.b])
                                S.op("pool", lambda e, z_=z_, x_=x_, xn_=xn_, fh=fh: e.tensor_tensor(
                                    out=xn_.t[:, fh * 512:(fh + 1) * 512], in0=z_.t[:], in1=x_.t[:, fh * 512:(fh + 1) * 512], op=ALU.add),
                                    reads=[z_.b, x_.b], writes=[xn_.b])
                            else:
                                S.op("act", lambda e, z_=z_, pz_=pz_: e.activation(out=z_.t[:], in_=pz_.t[:], func=AF.Copy),
                                     reads=[pz_.b], writes=[z_.b])
                                zB = Buf("zp")
                                zb.append(zB)
                                S.dma(lambda e, z_=z_, r0=r0, fh=fh: e.dma_start(
                                    out=Zp_s[r0:r0 + 128, fh * 512:(fh + 1) * 512], in_=z_.t[:]), reads=[z_.b], dram_writes=[zB], Q="act")
                        if cfg.split == 1:
                            if last:
                                dst = out_d[r0 - NCTX:r0 - NCTX + 128, :]
                            else:
                                dst = X_s[r0:r0 + 128, :]
                            S.dma(lambda e, xn_=xn_, dst=dst: e.dma_start(out=dst, in_=xn_.t[:]), reads=[xn_.b], Q="pool")
                        else:
                            S.coll(lambda e, r0=r0: e.collective_compute(
                                "AllGather", ALU.bypass, replica_groups=PAIRS,
                                ins=[Zp_s[r0:r0 + 128, :]], outs=[Zr_s[r0 // 128]]), reads=zb)
                if l + 1 < L and cfg.split == 1:
                    for _ in emit_M(l + 1, ph, pgt_slots=1):
                        pass
                S.end_phase()
            if cfg.split == 2:
                with ExitStack() as ph:
                    xt = sbn(ph, "xte", [128, D], F32, 3)
                    zr = sbn(ph, "zre", [128, 2, D], F32, 3)
                    xn = sbn(ph, "xne", [128, D], F32, 3)
                    for tb in range(NBLK):
                        isctx = tb < NCTX // 128
                        if isctx and not need_ctx:
                            continue
                        v = 1 if isctx else 0
                        r0 = tb * 128
                        x_, z_, xn_ = xt.next(), zr.next(), xn.next()
                        S.dma(lambda e, x_=x_, r0=r0: e.dma_start(out=x_.t[:], in_=xsrc[r0:r0 + 128, :]), writes=[x_.b])
                        S.dma(lambda e, z_=z_, tb=tb: e.dma_start(out=z_.t[:], in_=Zr_s[tb].rearrange("(r p) n -> p r n", p=128)),
                              writes=[z_.b])
                        S.op("dve", lambda e, z_=z_: e.tensor_tensor(out=z_.t[:, 0, :], in0=z_.t[:, 0, :], in1=z_.t[:, 1, :], op=ALU.add),
                             reads=[z_.b], writes=[z_.b])
                        S.op("dve", lambda e, z_=z_, v=v: e.tensor_tensor(out=z_.t[:, 0, :], in0=z_.t[:, 0, :], in1=gtb[v].t[:], op=ALU.mult),
                             reads=[z_.b, gtb[v].b], writes=[z_.b])
                        S.op("pool", lambda e, z_=z_, x_=x_, xn_=xn_: e.tensor_tensor(out=xn_.t[:], in0=z_.t[:, 0, :], in1=x_.t[:], op=ALU.add),
                             reads=[z_.b, x_.b], writes=[xn_.b])
                        if last:
                            dst = out_d[r0 - NCTX:r0 - NCTX + 128, :]
                        else:
                            dst = X_s[r0:r0 + 128, :]
                        S.dma(lambda e, xn_=xn_, dst=dst: e.dma_start(out=dst, in_=xn_.t[:]), reads=[xn_.b], Q="pool")
                    S.end_phase()
        print("instructions recorded:", S.n_instr, "dma sems:", S.n_dsem, "sbuf left:", nc.sbuf_bytes_remaining)
    return nc


_CACHE = {}


def run(cfg, inputs):
    key = (cfg.L, cfg.split, cfg.dbg, cfg.total_depth)
    if key not in _CACHE:
        _CACHE[key] = build_program(cfg)
    nc = _CACHE[key]
    in_maps = []
    for core in range(8):
        if cfg.split == 1:
            b, half = core % 4, 0
        else:
            b, half = core // 2, core % 2
        in_maps.append(prep_core_inputs(cfg, inputs, b, half))
    res = run_bass_kernel_spmd(nc, in_maps, core_ids=list(range(8)))
    return res


SPLIT = 2


def kernel(**inputs):
    cfg = Cfg(L=DEPTH, split=SPLIT)
    res = run(cfg, inputs)
    out = np.stack([np.asarray(res.results[b * SPLIT]["out"]) for b in range(4)], axis=0)
    return out.astype(np.float32)
```

```python
import numpy as np
import concourse.bass as bass
import concourse.mybir as mybir
from concourse.bass_utils import run_bass_kernel_spmd
from contextlib import ExitStack

F32 = mybir.dt.float32
BF16 = mybir.dt.bfloat16
ALU = mybir.AluOpType
AF = mybir.ActivationFunctionType

ENGS = ("pe", "act", "dve", "pool", "sp")

D = 1024
NCTX = 256
NLAT = 4096
T = NCTX + NLAT
NBLK = T // 128
HD = 64
EPS = 1e-6
DEPTH = 4


class Buf:
    __slots__ = ("name", "w", "r", "lsem", "ssem")

    def __init__(self, name=""):
        self.name = name
        self.w = None
        self.r = []
        self.lsem = None
        self.ssem = None


class TB:
    __slots__ = ("t", "b")

    def __init__(self, t, name=""):
        self.t = t
        self.b = Buf(name)


class Rot:
    def __init__(self, items):
        self.items = items
        self.i = 0

    def next(self):
        x = self.items[self.i % len(self.items)]
        self.i += 1
        return x


class Sched:
    def __init__(self, nc, stack, same_engine_sync=True):
        self.nc = nc
        self.stack = stack
        self.same = same_engine_sync
        self.esem = {e: stack.enter_context(nc.semaphore("es_" + e)) for e in ENGS if e != "sp"}
        self.cnt = {e: 0 for e in ENGS}
        self.seen = {e: {} for e in ENGS}
        self.prog = {e: [] for e in ENGS}
        self.free_dsems = []
        self.all_dsems = []
        self.n_dsem = 0
        self.dma_bufs = []
        self.n_instr = 0

    def _get_dsem(self, Q="sp"):
        kind = "sw" if Q == "pool" else "hw"
        for i, s in enumerate(self.free_dsems):
            if s[2] == kind:
                return self.free_dsems.pop(i)
        s = [self.stack.enter_context(self.nc.semaphore("ds%d" % self.n_dsem)), 0, kind]
        self.n_dsem += 1
        self.all_dsems.append(s)
        return s

    def _waits_for(self, E, reads, writes):
        evs = []
        for b in reads:
            if b.w is not None:
                evs.append(b.w)
        for b in writes:
            if b.w is not None:
                evs.append(b.w)
            evs.extend(b.r)
        out = []
        seen = self.seen[E]
        for ev in evs:
            if ev[0] == "e":
                _, X, k = ev
                if X == E and (not self.same or E == "pe"):
                    continue
                key = ("e", X)
                sem = self.esem[X]
            else:
                _, s, k = ev
                key = ("d", id(s))
                sem = s[0]
            if seen.get(key, 0) >= k:
                continue
            seen[key] = k
            out.append((sem, k))
        return out

    def op(self, E, fn, reads=(), writes=()):
        waits = self._waits_for(E, reads, writes)
        self.cnt[E] += 1
        ev = ("e", E, self.cnt[E])
        for b in reads:
            b.r.append(ev)
        for b in writes:
            b.w = ev
            b.r = []
        self.prog[E].append((waits, fn, (self.esem[E], 1)))
        self.n_instr += 1

    def coll(self, fn, reads=(), writes=(), key=0):
        waits = self._waits_for("pool", reads, writes)
        if not hasattr(self, "ccsems"):
            self.ccsems = {}
        if key not in self.ccsems:
            self.ccsems[key] = [self.stack.enter_context(self.nc.semaphore("cc%d" % key)), 0, "cc"]
            self.all_dsems.append(self.ccsems[key])
        s = self.ccsems[key]
        s[1] += 1
        ev = ("d", s, s[1])
        for b in reads:
            b.r.append(ev)
        for b in writes:
            b.w = ev
            b.r = []
        self.prog["pool"].append((waits, fn, (s[0], 1)))
        self.n_instr += 1

    def dma(self, fn, reads=(), writes=(), Q="sp", dram_reads=(), dram_writes=()):
        waits = self._waits_for(Q, list(reads) + list(dram_reads), list(writes) + list(dram_writes))
        if writes:
            b = writes[0]
            assert len(writes) == 1 and not reads
            if b.lsem is None or b.lsem[2] != ("sw" if Q == "pool" else "hw"):
                assert b.lsem is None
                b.lsem = self._get_dsem(Q)
                self.dma_bufs.append(b)
            s = b.lsem
        else:
            b = reads[0]
            assert len(reads) == 1
            if b.ssem is None:
                b.ssem = self._get_dsem(Q)
                self.dma_bufs.append(b)
            assert b.ssem[2] == ("sw" if Q == "pool" else "hw")
            s = b.ssem
        s[1] += 16
        ev = ("d", s, s[1])
        for b in list(reads) + list(dram_reads):
            b.r.append(ev)
        for b in list(writes) + list(dram_writes):
            b.w = ev
            b.r = []
        self.prog[Q].append((waits, fn, (s[0], 16)))
        self.n_instr += 1

    def barrier(self, skip_cc=False):
        for E in ENGS:
            waits = []
            seen = self.seen[E]
            for X in ENGS:
                if X == E or X == "sp" or self.cnt[X] == 0:
                    continue
                key = ("e", X)
                if seen.get(key, 0) < self.cnt[X]:
                    seen[key] = self.cnt[X]
                    waits.append((self.esem[X], self.cnt[X]))
            for s in self.all_dsems:
                if skip_cc and s[2] == "cc":
                    continue
                key = ("d", id(s))
                if s[1] > 0 and seen.get(key, 0) < s[1]:
                    seen[key] = s[1]
                    waits.append((s[0], s[1]))
            if waits:
                self.prog[E].append((waits, None, None))

    def end_phase(self, skip_cc=False):
        self.barrier(skip_cc)
        self.emit()
        for b in self.dma_bufs:
            for a in ("lsem", "ssem"):
                s = getattr(b, a)
                if s is not None:
                    self.free_dsems.append(s)
                    setattr(b, a, None)
        self.dma_bufs = []

    def emit(self):
        nc = self.nc
        prog = self.prog

        def run(E, eng):
            for waits, fn, inc in prog[E]:
                for sem, v in waits:
                    eng.wait_ge(sem, v)
                if fn is not None:
                    ins = fn(eng)
                    ins.then_inc(inc[0], inc[1])

        with nc.Block() as block:
            @block.sync
            def _(eng):
                run("sp", eng)

            @block.tensor
            def _(eng):
                run("pe", eng)

            @block.scalar
            def _(eng):
                run("act", eng)

            @block.vector
            def _(eng):
                run("dve", eng)

            @block.gpsimd
            def _(eng):
                run("pool", eng)
        self.prog = {e: [] for e in ENGS}


class Cfg:
    def __init__(self, L=DEPTH, split=1, dbg=False, total_depth=DEPTH):
        self.L = L
        self.split = split
        self.NQC = 8 // split
        self.NKC = 2 // split
        self.NLC = 8 // split
        self.NV = self.NKC * 128
        self.dbg = dbg
        self.total_depth = total_depth
        ch = []
        ch += [("q", j) for j in range(self.NQC)]
        ch += [("k", j) for j in range(self.NKC)]
        ch += [("ga", j) for j in range(self.NQC)]
        ch += [("u", j) for j in range(self.NLC)]
        ch += [("gb", j) for j in range(self.NLC)]
        ch += [("gm", j) for j in range(16)]
        ch += [("v", j) for j in range(self.NKC)]
        self.chunks = ch
        self.NCOLS = len(ch) * 128
        o = 0
        self.V_NG = o; o += 8
        self.V_BM = o; o += 24
        self.V_QG = o; o += 1
        self.V_KG = o; o += 1
        self.V_CW = o; o += 4 * self.NLC
        self.V_CB = o; o += self.NLC
        self.V_GB = o; o += 4 * self.NLC
        self.V_LM = o; o += 2 * self.NLC
        self.VL = o
        self.V_C = self.VL * L
        self.V_CX = self.V_C + 8
        self.NVEC = self.V_CX + 8

    def qheads(self, half):
        out = []
        if self.split == 1:
            for j in range(8):
                grp = (j // 4) * 2
                out.append((grp * 4 + j % 4, (grp + 1) * 4 + j % 4))
        else:
            for j in range(4):
                out.append((half * 8 + j, half * 8 + 4 + j))
        return out


def chunk_major(v, n):
    return np.ascontiguousarray(np.asarray(v, np.float32).reshape(n, 128).T)


def rope_tables():
    quarter = 16
    freqs = (10000.0 ** (-np.arange(quarter, dtype=np.float32) / quarter)).astype(np.float32)
    t = np.arange(NLAT)
    rows = (t // 64).astype(np.float32)
    cols = (t % 64).astype(np.float32)
    cosT = np.zeros((128, NLAT), np.float32)
    sinT = np.zeros((128, NLAT), np.float32)
    for p in range(128):
        d = p % 64
        pos = rows if d < 32 else cols
        dd = d % 32
        i = dd % 16
        ang = (pos * freqs[i]).astype(np.float32)
        cosT[p] = np.cos(ang)
        sinT[p] = (-np.sin(ang)) if dd < 16 else np.sin(ang)
    return np.stack([cosT, sinT])


def const_mats():
    ident = np.eye(128, dtype=np.float32)
    bones = np.zeros((128, 128), np.float32)
    bones[:64, :64] = 1.0
    bones[64:, 64:] = 1.0
    rm = np.zeros((128, 128), np.float32)
    for m in range(128):
        d = m % 64
        base = m - d
        dd = d % 32
        partner = d + 16 if dd < 16 else d - 16
        rm[base + partner, m] = 1.0
    return np.stack([ident, bones, rm])


def prep_core_inputs(cfg, inp, b, half):
    L = cfg.L
    qh = cfg.qheads(half)
    qcols = []
    for lo, hi in qh:
        qcols += list(range(lo * 64, lo * 64 + 64)) + list(range(hi * 64, hi * 64 + 64))
    qcols = np.array(qcols)
    if cfg.split == 1:
        kvh = [0, 1, 2, 3]
        lch = np.arange(1024)
    else:
        kvh = [2 * half, 2 * half + 1]
        lch = np.arange(512) + 512 * half
    kcols = np.concatenate([np.arange(h * 64, h * 64 + 64) for h in kvh])
    o_q, o_k, o_v, o_ga, o_u, o_gb, o_gm = 0, 1024, 1280, 1536, 2560, 3584, 4608
    cols = np.concatenate([o_q + qcols, o_k + kcols, o_ga + qcols, o_u + lch, o_gb + lch,
                           o_gm + np.arange(2048), o_v + kcols])
    assert len(cols) == cfg.NCOLS
    w_in = np.ascontiguousarray(np.asarray(inp["w_in"])[:L][:, :, cols])
    w_a = np.ascontiguousarray(np.asarray(inp["w_a_out"])[:L][:, qcols, :])
    w_b = np.ascontiguousarray(np.asarray(inp["w_b_out"])[:L][:, lch, :])
    w_o = np.ascontiguousarray(np.asarray(inp["w_out"])[:L])
    w_mod = np.ascontiguousarray(np.asarray(inp["w_mod"])[:L])
    nlc = cfg.NLC
    vecs = np.zeros((128, cfg.NVEC), np.float32)
    for l in range(L):
        o = l * cfg.VL
        vecs[:, o + cfg.V_NG:o + cfg.V_NG + 8] = chunk_major(inp["norm_g"][l], 8)
        vecs[:, o + cfg.V_BM:o + cfg.V_BM + 24] = chunk_major(inp["b_mod"][l], 24)
        vecs[:, o + cfg.V_QG] = np.tile(np.asarray(inp["q_norm_g"][l]), 2)
        vecs[:, o + cfg.V_KG] = np.tile(np.asarray(inp["k_norm_g"][l]), 2)
        for j in range(4):
            vecs[:, o + cfg.V_CW + j * nlc:o + cfg.V_CW + (j + 1) * nlc] = chunk_major(np.asarray(inp["conv_w"])[l, j][lch], nlc)
        vecs[:, o + cfg.V_CB:o + cfg.V_CB + nlc] = chunk_major(np.asarray(inp["conv_b"])[l][lch], nlc)
        for d in range(2):
            for g in range(2):
                k0 = o + cfg.V_GB + (d * 2 + g) * nlc
                vecs[:, k0:k0 + nlc] = chunk_major(np.asarray(inp["lru_gate_b"])[l, d, g][lch], nlc)
            k0 = o + cfg.V_LM + d * nlc
            vecs[:, k0:k0 + nlc] = chunk_major(np.asarray(inp["lru_lambda"])[l, d][lch], nlc)
    vecs[:, cfg.V_C:cfg.V_C + 8] = chunk_major(np.asarray(inp["c"])[b], 8)
    vecs[:, cfg.V_CX:cfg.V_CX + 8] = chunk_major(np.asarray(inp["c_ctx"]), 8)
    gw = np.asarray(inp["lru_gate_w"])[:L]
    if cfg.split == 2:
        gw = gw[:, :, :, 8 * half:8 * half + 8]
    gw = np.ascontiguousarray(gw)
    xc = np.ascontiguousarray(np.concatenate([np.asarray(inp["ctx"])[b], np.asarray(inp["x"])[b]], axis=0))
    return {"xc": xc, "w_in": w_in, "w_mod": w_mod, "w_a": w_a, "w_b": w_b, "w_o": w_o,
            "vecs": vecs, "gate_w": gw, "rope": rope_tables(), "cmats": const_mats()}


def build_program(cfg):
    nc = bass.Bass("TRN2", target_bir_lowering=False)
    L, NQC, NKC, NLC, NV = cfg.L, cfg.NQC, cfg.NKC, cfg.NLC, cfg.NV
    NCOLS = cfg.NCOLS

    def dram_in(name, shape, dt=F32):
        return nc.dram_tensor(name, list(shape), dt, kind="ExternalInput").ap()

    skind = "ExternalOutput" if cfg.dbg else "Internal"

    def dram_scr(name, shape, dt):
        return nc.dram_tensor(name, list(shape), dt, kind=skind).ap()

    xc_d = dram_in("xc", [T, D])
    w_in_d = dram_in("w_in", [L, D, NCOLS])
    w_mod_d = dram_in("w_mod", [L, D, 3 * D])
    w_a_d = dram_in("w_a", [L, NQC * 128, D])
    w_b_d = dram_in("w_b", [L, NLC * 128, D])
    w_o_d = dram_in("w_o", [L, D, D])
    vecs_d = dram_in("vecs", [128, cfg.NVEC])
    gate_w_d = dram_in("gate_w", [L, 2, 2, 2 * NLC, 64, 64])
    rope_d = dram_in("rope", [2, 128, NLAT])
    cmats_d = dram_in("cmats", [3, 128, 128])
    out_d = nc.dram_tensor("out", [NLAT, D], F32, kind="ExternalOutput").ap()

    X_s = dram_scr("X_s", [T, D], F32)
    QT_s = dram_scr("QT_s", [NQC, 128, T], BF16)
    KT_s = dram_scr("KT_s", [NKC, 128, T], BF16)
    V_s = dram_scr("V_s", [T, NV], BF16)
    SGA_s = dram_scr("SGA_s", [NQC, 128, T], BF16)
    UT_s = dram_scr("UT_s", [NLC, 128, T], F32)
    SGB_s = dram_scr("SGB_s", [NLC, 128, T], BF16)
    SGM_s = dram_scr("SGM_s", [16, 128, T], BF16)
    OG_s = dram_scr("OG_s", [NQC, 128, T], BF16)
    HG_s = dram_scr("HG_s", [NLC, 128, T], BF16)
    if cfg.split == 2:
        Zp_s = nc.dram_tensor("Zp_s", [T, D], F32).ap()
        Zr_s = nc.dram_tensor("Zr_s", [NBLK, 256, D], F32).ap()
    PAIRS = [[0, 1], [2, 3], [4, 5], [6, 7]]

    tiles = [(0, NCTX, 1)] + [(NCTX + 512 * i, 512, 0) for i in range(NLAT // 512)]

    with ExitStack() as top:
        S = Sched(nc, top)

        uid = [0]

        def un(name):
            uid[0] += 1
            return "%s_%d" % (name, uid[0])

        def sb(stack, name, shape, dt=F32):
            return TB(stack.enter_context(nc.sbuf_tensor(un(name), list(shape), dt)), name)

        def sbn(stack, name, shape, dt, n):
            return Rot([sb(stack, "%s%d" % (name, i), shape, dt) for i in range(n)])

        def ps(stack, name, shape=(128, 512), dt=F32):
            return TB(stack.enter_context(nc.psum_tensor(un(name), list(shape), dt)), name)

        def psn(stack, name, n, shape=(128, 512)):
            return Rot([ps(stack, "%s%d" % (name, i), shape) for i in range(n)])

        cm = sb(top, "cm", [128, 3, 128])
        ones = sb(top, "ones", [128, 128])
        epsb = sb(top, "epsb", [128, 1])
        bonesb = sb(top, "bonesb", [128, 128], BF16)
        epsx = sb(top, "epsx", [128, 1])
        vecs = sb(top, "vecs", [128, cfg.NVEC])
        sc2 = sb(top, "sc2", [128, 8, 2])
        MS = []
        for par in range(2):
            MS.append((sb(top, "mod", [128, 24, 2]), sb(top, "gmod", [128, 8, 2]),
                       [sb(top, "gtb%d" % v, [128, D]) for v in range(2)],
                       sb(top, "gk8", [128, 1]), sb(top, "scd", [128, 2 * NLC])))
        wg32 = nc_wg32 = top.enter_context(nc.sbuf_tensor(un("wg32"), [128, 4, 128], F32))
        wgB = [[Buf("wg%d%d" % (h, dg)) for dg in range(4)] for h in range(2)]

        ident = cm.t[:, 0, :]
        bones = cm.t[:, 1, :]
        rmat = cm.t[:, 2, :]

        S.dma(lambda e: e.dma_start(out=cm.t[:], in_=cmats_d.rearrange("c p n -> p c n")), writes=[cm.b])
        S.dma(lambda e: e.dma_start(out=vecs.t[:], in_=vecs_d), writes=[vecs.b])
        S.op("pool", lambda e: e.memset(ones.t[:], 1.0), writes=[ones.b])
        S.op("pool", lambda e: e.memset(epsb.t[:], HD * EPS), writes=[epsb.b])
        S.op("pool", lambda e: e.memset(epsx.t[:], EPS), writes=[epsx.b])
        S.op("pool", lambda e: e.memset(wg32[:], 0.0), writes=[b for row in wgB for b in row])
        S.op("pool", lambda e: e.tensor_copy(out=bonesb.t[:], in_=cm.t[:, 1, :]), reads=[cm.b], writes=[bonesb.b])
        for v, col in ((0, cfg.V_C), (1, cfg.V_CX)):
            S.op("act", lambda e, v=v, col=col: e.activation(out=sc2.t[:, :, v], in_=vecs.t[:, col:col + 8], func=AF.Silu),
                 reads=[vecs.b], writes=[sc2.b])
        S.end_phase()

        def emit_M(lm, ph, pgt_slots=2):
            mod, gmod, gtb, gk8, scd = MS[lm % 2]
            vo_m = lm * cfg.VL

            def vcol(off, n=1):
                return vecs.t[:, vo_m + off:vo_m + off + n]

            wst = sbn(ph, "wstm", [128, 8, 512], F32, 2)
            pmod = ps(ph, "pmod", [128, 24, 2])
            pgt = psn(ph, "pgt", pgt_slots)
            grep = sbn(ph, "grep", [128, 128], F32, 2)
            for g in range(6):
                w = wst.next()
                S.dma(lambda e, w=w, g=g: e.dma_start(
                    out=w.t[:], in_=w_mod_d[lm, :, g * 512:(g + 1) * 512].rearrange("(kc p) n -> p kc n", p=128)),
                    writes=[w.b])
                for f4 in range(4):
                    fc = g * 4 + f4
                    for kc in range(8):
                        S.op("pe", lambda e, w=w, f4=f4, fc=fc, kc=kc: e.matmul(
                            pmod.t[:, fc, :], lhsT=w.t[:, kc, f4 * 128:(f4 + 1) * 128], rhs=sc2.t[:, kc, :],
                            start=(kc == 0), stop=(kc == 7)), reads=[w.b, sc2.b], writes=[pmod.b])
                yield
            for v in range(2):
                S.op("dve", lambda e, v=v: e.tensor_tensor(out=mod.t[:, :, v], in0=pmod.t[:, :, v],
                                                           in1=vcol(cfg.V_BM, 24), op=ALU.add),
                     reads=[pmod.b, vecs.b], writes=[mod.b])
            for v in range(2):
                S.op("dve", lambda e, v=v: e.scalar_tensor_tensor(
                    out=gmod.t[:, :, v], in0=mod.t[:, 8:16, v], scalar=1.0, in1=vcol(cfg.V_NG, 8),
                    op0=ALU.add, op1=ALU.mult), reads=[mod.b, vecs.b], writes=[gmod.b])
            for v in range(2):
                for kc in range(8):
                    gr = grep.next()
                    pg = pgt.next()
                    S.op("dve", lambda e, gr=gr, kc=kc, v=v: e.tensor_scalar(
                        out=gr.t[:], in0=ones.t[:], scalar1=mod.t[:, 16 + kc, v:v + 1], scalar2=None, op0=ALU.mult),
                        reads=[ones.b, mod.b], writes=[gr.b])
                    S.op("pe", lambda e, gr=gr, pg=pg: e.matmul(pg.t[:, 0:128], lhsT=gr.t[:], rhs=ident,
                                                               start=True, stop=True),
                         reads=[gr.b, cm.b], writes=[pg.b])
                    S.op("dve", lambda e, pg=pg, kc=kc, v=v: e.tensor_copy(
                        out=gtb[v].t[:, kc * 128:(kc + 1) * 128], in_=pg.t[:, 0:128]),
                        reads=[pg.b], writes=[gtb[v].b])
            S.op("dve", lambda e: e.tensor_scalar(out=gk8.t[:], in0=vcol(cfg.V_KG), scalar1=8.0, scalar2=None,
                                                  op0=ALU.mult), reads=[vecs.b], writes=[gk8.b])
            S.op("act", lambda e: e.activation(out=scd.t[:], in_=vcol(cfg.V_LM, 2 * NLC), func=AF.Exp, scale=-1.0),
                 reads=[vecs.b], writes=[scd.b])
            S.op("act", lambda e: e.activation(out=scd.t[:], in_=scd.t[:], func=AF.Ln, bias=1.0, scale=1.0),
                 reads=[scd.b], writes=[scd.b])
            S.op("dve", lambda e: e.tensor_scalar(out=scd.t[:], in0=scd.t[:], scalar1=-8.0, scalar2=None, op0=ALU.mult),
                 reads=[scd.b], writes=[scd.b])


        with ExitStack() as ph:
            for _ in emit_M(0, ph):
                pass
            S.end_phase()

        ZrB = [Buf("zr%d" % i) for i in range(NBLK)]
        for l in range(L):
            mod, gmod, gtb, gk8, scd = MS[l % 2]
            need_ctx = l < cfg.total_depth - 1
            last = l == cfg.total_depth - 1
            xsrc = xc_d if l == 0 else X_s
            vo = l * cfg.VL

            def vcol(off, n=1, vo=vo):
                return vecs.t[:, vo + off:vo + off + n]

            with ExitStack() as ph:
                hT = ph.enter_context(nc.sbuf_tensor(un("hT"), [128, 8, T], BF16))
                hB = [Buf("hT%d" % i) for i in range(NBLK)]
                wst = sbn(ph, "wst", [128, 8, 512], F32, 2)
                wbf = sbn(ph, "wbf", [128, 8, 512], BF16, 2)
                pmm = psn(ph, "pmm", 3)
                with ExitStack() as ph1:
                    xt = sbn(ph1, "xt", [128, D], F32, 3)
                    xs = sbn(ph1, "xs", [128, D], F32, 2)
                    junk = sbn(ph1, "junk", [128, D], BF16, 2)
                    ss = sbn(ph1, "ss", [128, 1], F32, 3)
                    rs = sbn(ph1, "rs", [128, 1], F32, 3)
                    ptr = psn(ph1, "ptr", 2)
                    genM = None
                    for tb in range(NBLK):
                        if genM is not None and tb % 5 == 2:
                            next(genM, None)
                        v = 1 if tb < NCTX // 128 else 0
                        x_, xs_, jk, ss_, rs_ = xt.next(), xs.next(), junk.next(), ss.next(), rs.next()
                        S.dma(lambda e, x_=x_, tb=tb: e.dma_start(out=x_.t[:], in_=xsrc[tb * 128:(tb + 1) * 128, :]),
                              writes=[x_.b])
                        S.op("act", lambda e, x_=x_, jk=jk, ss_=ss_: e.activation(
                            out=jk.t[:], in_=x_.t[:], func=AF.Square, accum_out=ss_.t[:, 0:1]),
                            reads=[x_.b], writes=[jk.b, ss_.b])
                        S.op("act", lambda e, ss_=ss_, rs_=rs_: e.activation(
                            out=rs_.t[:], in_=ss_.t[:], func=AF.Sqrt, bias=epsx.t[:, 0:1], scale=1.0 / D),
                            reads=[ss_.b, epsx.b], writes=[rs_.b])
                        S.op("dve", lambda e, rs_=rs_: e.reciprocal(out=rs_.t[:], in_=rs_.t[:]),
                            reads=[rs_.b], writes=[rs_.b])
                        S.op("dve", lambda e, x_=x_, xs_=xs_, rs_=rs_: e.tensor_scalar(
                            out=xs_.t[:], in0=x_.t[:], scalar1=rs_.t[:, 0:1], scalar2=None, op0=ALU.mult),
                            reads=[x_.b, rs_.b], writes=[xs_.b])
                        for half in range(2):
                            p_ = ptr.next()
                            for q4 in range(4):
                                kc = half * 4 + q4
                                S.op("pe", lambda e, p_=p_, xs_=xs_, q4=q4, kc=kc: e.transpose(
                                    out=p_.t[:, q4 * 128:(q4 + 1) * 128], in_=xs_.t[:, kc * 128:(kc + 1) * 128], identity=ident),
                                    reads=[xs_.b, cm.b], writes=[p_.b])
                            for q4 in range(4):
                                kc = half * 4 + q4
                                S.op("act", lambda e, p_=p_, q4=q4, kc=kc, tb=tb, v=v: e.activation(
                                    out=hT[:, kc, tb * 128:(tb + 1) * 128], in_=p_.t[:, q4 * 128:(q4 + 1) * 128],
                                    func=AF.Identity, bias=mod.t[:, kc, v:v + 1], scale=gmod.t[:, kc, v:v + 1]),
                                    reads=[p_.b, mod.b, gmod.b], writes=[hB[tb]])
                    if genM is not None:
                        for _ in genM:
                            pass
                with ExitStack() as ph2:
                    sq = sbn(ph2, "sq", [128, 512], BF16, 3)
                    rstd = sbn(ph2, "rstd", [128, 512], F32, 2)
                    qn = sbn(ph2, "qn", [128, 512], F32, 3)
                    t1 = sbn(ph2, "t1", [128, 512], F32, 2)
                    t2 = sbn(ph2, "t2", [128, 512], F32, 2)
                    qr = sbn(ph2, "qr", [128, 512], BF16, 3)
                    o16 = sbn(ph2, "o16", [128, 512], BF16, 3)
                    o32 = sbn(ph2, "o32", [128, 512], F32, 2)
                    rC = sbn(ph2, "rC", [128, 512], F32, 3)
                    rS = sbn(ph2, "rS", [128, 512], F32, 3)
                    vt = sbn(ph2, "vt", [128, NV], BF16, 2)
                    pss = psn(ph2, "pss", 2)
                    prot = psn(ph2, "prot", 2)

                    def qk_stage0(u):
                        pm = u["pm"] = pmm.next()
                        wb_, c4, t0, n = u["wb"], u["c4"], u["t0"], u["n"]
                        hb = [hB[i] for i in range(t0 // 128, (t0 + n) // 128)]
                        for kc in range(8):
                            S.op("pe", lambda e, pm=pm, wb_=wb_, kc=kc, c4=c4, t0=t0, n=n: e.matmul(
                                pm.t[:, 0:n], lhsT=wb_.t[:, kc, c4 * 128:(c4 + 1) * 128], rhs=hT[:, kc, t0:t0 + n],
                                start=(kc == 0), stop=(kc == 7)), reads=hb + [wb_.b], writes=[pm.b])
                        sq_ = u["sq"] = sq.next()
                        S.op("act", lambda e, sq_=sq_, pm=pm, n=n: e.activation(
                            out=sq_.t[:, 0:n], in_=pm.t[:, 0:n], func=AF.Square), reads=[pm.b], writes=[sq_.b])
                        if not u["isctx"]:
                            rC_, rS_ = u["rC"], u["rS"] = rC.next(), rS.next()
                            l0 = t0 - NCTX
                            S.dma(lambda e, rC_=rC_, l0=l0, n=n: e.dma_start(out=rC_.t[:, 0:n], in_=rope_d[0, :, l0:l0 + n]),
                                  writes=[rC_.b])
                            S.dma(lambda e, rS_=rS_, l0=l0, n=n: e.dma_start(out=rS_.t[:, 0:n], in_=rope_d[1, :, l0:l0 + n]),
                                  writes=[rS_.b])

                    def qk_stage1(u):
                        pm, sq_, n, typ = u["pm"], u["sq"], u["n"], u["typ"]
                        gvec = vcol(cfg.V_QG) if typ == "q" else gk8.t[:, 0:1]
                        gbuf = vecs.b if typ == "q" else gk8.b
                        pss_, rstd_ = pss.next(), rstd.next()
                        S.op("pe", lambda e, pss_=pss_, sq_=sq_, n=n: e.matmul(
                            pss_.t[:, 0:n], lhsT=bonesb.t[:], rhs=sq_.t[:, 0:n], start=True, stop=True),
                            reads=[sq_.b, bonesb.b], writes=[pss_.b])
                        S.op("act", lambda e, rstd_=rstd_, pss_=pss_, n=n: e.activation(
                            out=rstd_.t[:, 0:n], in_=pss_.t[:, 0:n], func=AF.Ln, bias=epsb.t[:, 0:1], scale=1.0),
                            reads=[pss_.b, epsb.b], writes=[rstd_.b])
                        S.op("act", lambda e, rstd_=rstd_, n=n: e.activation(
                            out=rstd_.t[:, 0:n], in_=rstd_.t[:, 0:n], func=AF.Exp, scale=-0.5),
                            reads=[rstd_.b], writes=[rstd_.b])
                        if u["isctx"]:
                            o_ = u["q"] = qr.next()
                        else:
                            o_ = u["qn"] = qn.next()
                        S.op("dve", lambda e, o_=o_, pm=pm, rstd_=rstd_, n=n, gvec=gvec: e.scalar_tensor_tensor(
                            out=o_.t[:, 0:n], in0=pm.t[:, 0:n], scalar=gvec, in1=rstd_.t[:, 0:n],
                            op0=ALU.mult, op1=ALU.mult), reads=[pm.b, rstd_.b, gbuf], writes=[o_.b])

                    def qk_stage2(u):
                        n, t0, typ, idx = u["n"], u["t0"], u["typ"], u["idx"]
                        dst = QT_s if typ == "q" else KT_s
                        if not u["isctx"]:
                            qn_, rC_, rS_ = u["qn"], u["rC"], u["rS"]
                            pr_ = prot.next()
                            S.op("pe", lambda e, pr_=pr_, qn_=qn_, n=n: e.matmul(
                                pr_.t[:, 0:n], lhsT=rmat, rhs=qn_.t[:, 0:n], start=True, stop=True),
                                reads=[qn_.b, cm.b], writes=[pr_.b])
                            t1_, t2_, q_ = t1.next(), t2.next(), qr.next()
                            S.op("pool", lambda e, t1_=t1_, qn_=qn_, rC_=rC_, n=n: e.tensor_tensor(
                                out=t1_.t[:, 0:n], in0=qn_.t[:, 0:n], in1=rC_.t[:, 0:n], op=ALU.mult),
                                reads=[qn_.b, rC_.b], writes=[t1_.b])
                            S.op("dve", lambda e, t2_=t2_, pr_=pr_, rS_=rS_, n=n: e.tensor_tensor(
                                out=t2_.t[:, 0:n], in0=pr_.t[:, 0:n], in1=rS_.t[:, 0:n], op=ALU.mult),
                                reads=[pr_.b, rS_.b], writes=[t2_.b])
                            S.op("pool", lambda e, q_=q_, t1_=t1_, t2_=t2_, n=n: e.tensor_tensor(
                                out=q_.t[:, 0:n], in0=t1_.t[:, 0:n], in1=t2_.t[:, 0:n], op=ALU.add),
                                reads=[t1_.b, t2_.b], writes=[q_.b])
                        else:
                            q_ = u["q"]
                        S.dma(lambda e, q_=q_, dst=dst, idx=idx, t0=t0, n=n: e.dma_start(
                            out=dst[idx, :, t0:t0 + n], in_=q_.t[:, 0:n]), reads=[q_.b], Q="pool")

                    fl = [None, None]

                    def qk_step(u):
                        if u is not None:
                            qk_stage0(u)
                        if fl[0] is not None:
                            qk_stage1(fl[0])
                        if fl[1] is not None:
                            qk_stage2(fl[1])
                        fl[1] = fl[0]
                        fl[0] = u

                    def qk_flush():
                        if fl[0] is not None or fl[1] is not None:
                            qk_step(None)
                            qk_step(None)

                    ngroups = (NCOLS + 511) // 512
                    for g in range(ngroups):
                        c0 = g * 512
                        ncol = min(512, NCOLS - c0)
                        w, wb_ = wst.next(), wbf.next()
                        S.dma(lambda e, w=w, c0=c0, ncol=ncol: e.dma_start(
                            out=w.t[:, :, 0:ncol], in_=w_in_d[l, :, c0:c0 + ncol].rearrange("(kc p) n -> p kc n", p=128)),
                            writes=[w.b])
                        S.op("pool", lambda e, w=w, wb_=wb_, ncol=ncol: e.tensor_copy(out=wb_.t[:, :, 0:ncol], in_=w.t[:, :, 0:ncol]),
                             reads=[w.b], writes=[wb_.b])
                        done_v = False
                        for c4 in range(ncol // 128):
                            typ, idx = cfg.chunks[g * 4 + c4]
                            if typ == "v":
                                if done_v:
                                    continue
                                done_v = True
                                assert c4 * 128 + NV <= ncol
                                for tb in range(NBLK):
                                    pv = pmm.next()
                                    for kc in range(8):
                                        S.op("pe", lambda e, pv=pv, wb_=wb_, kc=kc, tb=tb, c4=c4: e.matmul(
                                            pv.t[:, 0:NV], lhsT=hT[:, kc, tb * 128:(tb + 1) * 128],
                                            rhs=wb_.t[:, kc, c4 * 128:c4 * 128 + NV], start=(kc == 0), stop=(kc == 7)),
                                            reads=[hB[tb], wb_.b], writes=[pv.b])
                                    v_ = vt.next()
                                    S.op("act", lambda e, pv=pv, v_=v_: e.activation(out=v_.t[:], in_=pv.t[:, 0:NV], func=AF.Copy),
                                         reads=[pv.b], writes=[v_.b])
                                    S.dma(lambda e, v_=v_, tb=tb: e.dma_start(out=V_s[tb * 128:(tb + 1) * 128, :], in_=v_.t[:]),
                                          reads=[v_.b], Q="act")
                                continue
                            if typ in ("q", "k"):
                                for (t0, n, isctx) in tiles:
                                    qk_step({"typ": typ, "idx": idx, "t0": t0, "n": n, "isctx": isctx, "wb": wb_, "c4": c4})
                                continue
                            qk_flush()
                            for (t0, n, isctx) in tiles:
                                pm = pmm.next()
                                hb = [hB[i] for i in range(t0 // 128, (t0 + n) // 128)]
                                for kc in range(8):
                                    S.op("pe", lambda e, pm=pm, wb_=wb_, kc=kc, c4=c4, t0=t0, n=n: e.matmul(
                                        pm.t[:, 0:n], lhsT=wb_.t[:, kc, c4 * 128:(c4 + 1) * 128], rhs=hT[:, kc, t0:t0 + n],
                                        start=(kc == 0), stop=(kc == 7)), reads=hb + [wb_.b], writes=[pm.b])
                                if typ in ("ga", "gb", "gm"):
                                    o_ = o16.next()
                                    fn = AF.Sigmoid if typ == "gm" else AF.Silu
                                    dst = {"ga": SGA_s, "gb": SGB_s, "gm": SGM_s}[typ]
                                    S.op("act", lambda e, o_=o_, pm=pm, n=n, fn=fn: e.activation(
                                        out=o_.t[:, 0:n], in_=pm.t[:, 0:n], func=fn), reads=[pm.b], writes=[o_.b])
                                    S.dma(lambda e, o_=o_, dst=dst, idx=idx, t0=t0, n=n: e.dma_start(
                                        out=dst[idx, :, t0:t0 + n], in_=o_.t[:, 0:n]), reads=[o_.b], Q="act")
                                else:
                                    assert typ == "u"
                                    o_ = o32.next()
                                    S.op("act", lambda e, o_=o_, pm=pm, n=n: e.activation(
                                        out=o_.t[:, 0:n], in_=pm.t[:, 0:n], func=AF.Copy), reads=[pm.b], writes=[o_.b])
                                    S.dma(lambda e, o_=o_, idx=idx, t0=t0, n=n: e.dma_start(
                                        out=UT_s[idx, :, t0:t0 + n], in_=o_.t[:, 0:n]), reads=[o_.b], Q="act")
                    qk_flush()
                S.end_phase()

            with ExitStack() as ph:
                u_ = sb(ph, "u", [128, T])
                uc = sb(ph, "uc", [128, T])
                ucb = sb(ph, "ucb", [128, T], BF16)
                ad = [sb(ph, "a%d" % d, [128, T]) for d in range(2)]
                bd = [sb(ph, "bb%d" % d, [128, T]) for d in range(2)]
                hd = [sb(ph, "hd%d" % d, [128, T]) for d in range(2)]
                sgb = sb(ph, "sgb", [128, T], BF16)
                hg = sb(ph, "hg", [128, T], BF16)
                wgbf = sb(ph, "wgbf", [128, 4, 128], BF16)
                tt_ = sbn(ph, "tt", [128, 512], F32, 2)
                mm_ = sbn(ph, "mm", [128, 512], F32, 2)
                bi = sbn(ph, "bi", [128, 512], F32, 2)
                pr = psn(ph, "pr", 2)
                pi = psn(ph, "pi", 2)
                aBd = [[Buf("aB%d" % i) for i in range(len(tiles))] for d in range(2)]
                bBd = [[Buf("bB%d" % i) for i in range(len(tiles))] for d in range(2)]
                pstep = ad[0].t[:].ap[0][0]

                def revap(tns, t_last, n):
                    return bass.AP(tns, t_last, [[pstep, 128], [-1, n]])

                for cc in range(NLC):
                    S.dma(lambda e, cc=cc: e.dma_start(out=u_.t[:], in_=UT_s[cc]), writes=[u_.b])
                    S.dma(lambda e, cc=cc: e.dma_start(out=sgb.t[:], in_=SGB_s[cc]), writes=[sgb.b])
                    for d in range(2):
                        for g in range(2):
                            for h in range(2):
                                S.dma(lambda e, d=d, g=g, h=h, cc=cc: e.dma_start(
                                    out=wg32[h * 64:(h + 1) * 64, d * 2 + g, h * 64:(h + 1) * 64],
                                    in_=gate_w_d[l, d, g, 2 * cc + h]), writes=[wgB[h][d * 2 + g]])
                    S.op("pool", lambda e: e.tensor_copy(out=wgbf.t[:], in_=wg32[:]),
                         reads=[b for row in wgB for b in row], writes=[wgbf.b])
                    cw = lambda j, cc=cc: vcol(cfg.V_CW + j * NLC + cc)
                    S.op("dve", lambda e, cc=cc, cw=cw: e.tensor_scalar(
                        out=uc.t[:], in0=u_.t[:], scalar1=cw(2), scalar2=vcol(cfg.V_CB + cc), op0=ALU.mult, op1=ALU.add),
                        reads=[u_.b, vecs.b], writes=[uc.b])
                    for (s0, e0) in ((0, NCTX), (NCTX, T)):
                        for (j, do, di, ln) in ((0, 2, 0, e0 - s0 - 2), (1, 1, 0, e0 - s0 - 1), (3, 0, 1, e0 - s0 - 1)):
                            S.op("dve", lambda e, s0=s0, j=j, do=do, di=di, ln=ln, cw=cw: e.scalar_tensor_tensor(
                                out=uc.t[:, s0 + do:s0 + do + ln], in0=u_.t[:, s0 + di:s0 + di + ln], scalar=cw(j),
                                in1=uc.t[:, s0 + do:s0 + do + ln], op0=ALU.mult, op1=ALU.add),
                                reads=[u_.b, uc.b, vecs.b], writes=[uc.b])
                    S.op("act", lambda e: e.activation(out=ucb.t[:], in_=uc.t[:], func=AF.Copy), reads=[uc.b], writes=[ucb.b])
                    for d in range(2):
                        cr = cfg.V_GB + (d * 2 + 0) * NLC + cc
                        ci = cfg.V_GB + (d * 2 + 1) * NLC + cc
                        a_, bb, aB, bB = ad[d], bd[d], aBd[d], bBd[d]
                        for ti, (t0, n, isctx) in enumerate(tiles):
                            pr_, pi_ = pr.next(), pi.next()
                            S.op("pe", lambda e, pr_=pr_, d=d, t0=t0, n=n: e.matmul(
                                pr_.t[:, 0:n], lhsT=wgbf.t[:, d * 2 + 0, :], rhs=ucb.t[:, t0:t0 + n], start=True, stop=True),
                                reads=[wgbf.b, ucb.b], writes=[pr_.b])
                            S.op("pe", lambda e, pi_=pi_, d=d, t0=t0, n=n: e.matmul(
                                pi_.t[:, 0:n], lhsT=wgbf.t[:, d * 2 + 1, :], rhs=ucb.t[:, t0:t0 + n], start=True, stop=True),
                                reads=[wgbf.b, ucb.b], writes=[pi_.b])
                            S.op("act", lambda e, pr_=pr_, t0=t0, n=n, cr=cr, a_=a_: e.activation(
                                out=a_.t[:, t0:t0 + n], in_=pr_.t[:, 0:n], func=AF.Sigmoid, bias=vcol(cr), scale=1.0),
                                reads=[pr_.b, vecs.b], writes=[aB[ti]])
                            S.op("act", lambda e, pi_=pi_, t0=t0, n=n, ci=ci, bb=bb: e.activation(
                                out=bb.t[:, t0:t0 + n], in_=pi_.t[:, 0:n], func=AF.Sigmoid, bias=vcol(ci), scale=1.0),
                                reads=[pi_.b, vecs.b], writes=[bB[ti]])
                    for d in range(2):
                        sidx = d * NLC + cc
                        a_, aB = ad[d], aBd[d]
                        for ti, (t0, n, isctx) in enumerate(tiles):
                            S.op("act", lambda e, t0=t0, n=n, sidx=sidx, a_=a_: e.activation(
                                out=a_.t[:, t0:t0 + n], in_=a_.t[:, t0:t0 + n], func=AF.Exp, scale=scd.t[:, sidx:sidx + 1]),
                                reads=[aB[ti], scd.b], writes=[aB[ti]])
                    for d in range(2):
                        a_, bb, aB, bB = ad[d], bd[d], aBd[d], bBd[d]
                        for ti, (t0, n, isctx) in enumerate(tiles):
                            t_, m_, bi_ = tt_.next(), mm_.next(), bi.next()
                            S.op("act", lambda e, t_=t_, t0=t0, n=n, a_=a_: e.activation(
                                out=t_.t[:, 0:n], in_=a_.t[:, t0:t0 + n], func=AF.Square),
                                reads=[aB[ti]], writes=[t_.b])
                            S.op("act", lambda e, m_=m_, t_=t_, n=n: e.activation(
                                out=m_.t[:, 0:n], in_=t_.t[:, 0:n], func=AF.Sqrt, bias=1.0, scale=-1.0),
                                reads=[t_.b], writes=[m_.b])
                            S.op("dve", lambda e, bi_=bi_, t0=t0, n=n, bb=bb: e.tensor_tensor(
                                out=bi_.t[:, 0:n], in0=bb.t[:, t0:t0 + n], in1=uc.t[:, t0:t0 + n], op=ALU.mult),
                                reads=[bB[ti], uc.b], writes=[bi_.b])
                            S.op("pool", lambda e, bi_=bi_, m_=m_, t0=t0, n=n, bb=bb: e.tensor_tensor(
                                out=bb.t[:, t0:t0 + n], in0=bi_.t[:, 0:n], in1=m_.t[:, 0:n], op=ALU.mult),
                                reads=[bi_.b, m_.b], writes=[bB[ti]])
                        h_ = hd[d]
                        if d == 0:
                            S.op("dve", lambda e, h_=h_, a_=a_, bb=bb: e.tensor_tensor_scan(
                                out=h_.t[:], data0=a_.t[:], data1=bb.t[:], initial=0.0, op0=ALU.mult, op1=ALU.add),
                                reads=aB + bB, writes=[h_.b])
                        else:
                            S.op("dve", lambda e, h_=h_, a_=a_, bb=bb: e.tensor_tensor_scan(
                                out=revap(h_.t, NCTX - 1, NCTX), data0=revap(a_.t, NCTX - 1, NCTX),
                                data1=revap(bb.t, NCTX - 1, NCTX), initial=0.0, op0=ALU.mult, op1=ALU.add),
                                reads=aB + bB, writes=[h_.b])
                            S.op("dve", lambda e, h_=h_, a_=a_, bb=bb: e.tensor_tensor_scan(
                                out=revap(h_.t, T - 1, NLAT), data0=revap(a_.t, T - 1, NLAT),
                                data1=revap(bb.t, T - 1, NLAT), initial=h_.t[:, 0:1], op0=ALU.mult, op1=ALU.add),
                                reads=aB + bB + [h_.b], writes=[h_.b])
                    S.op("dve", lambda e: e.tensor_tensor(out=hd[0].t[:], in0=hd[0].t[:], in1=hd[1].t[:], op=ALU.add),
                         reads=[hd[0].b, hd[1].b], writes=[hd[0].b])
                    S.op("pool", lambda e: e.tensor_tensor(out=hg.t[:], in0=hd[0].t[:], in1=sgb.t[:], op=ALU.mult),
                         reads=[hd[0].b, sgb.b], writes=[hg.b])
                    S.dma(lambda e, cc=cc: e.dma_start(out=HG_s[cc], in_=hg.t[:]), reads=[hg.b], Q="pool")
                S.end_phase()

            with ExitStack() as ph:
                VW = NKC * 256 + 64
                kt = sb(ph, "kt", [128, NKC, T], BF16)
                vp = ph.enter_context(nc.sbuf_tensor(un("vp"), [128, NBLK, VW], BF16))
                vpB = Buf("vp")
                qT = sbn(ph, "qT", [128, 2, 512], BF16, 2)
                for q_ in qT.items:
                    S.op("pool", lambda e, q_=q_: e.memset(q_.t[:], 0.0), writes=[q_.b])
                sga = sbn(ph, "sga", [128, 512], BF16, 2)
                P = sbn(ph, "P", [128, 1024], BF16, 4)
                rd = sbn(ph, "rd", [128, 512], F32, 2)
                og32 = sbn(ph, "og32", [128, 512], F32, 2)
                ogt = sbn(ph, "ogt", [128, 512], BF16, 2)
                ocp = [sbn(ph, "ocp%d" % hh, [128, 512], F32, 2) for hh in range(2)]
                Sps = psn(ph, "S", 3, shape=(128, 1024))
                Ops = [ps(ph, "O0"), ps(ph, "O1")]
                LOOK = 2
                S.dma(lambda e: e.dma_start(out=kt.t[:], in_=KT_s.rearrange("c p t -> p c t")), writes=[kt.b])
                S.op("pool", lambda e: e.memset(vp[:], 1.0), writes=[vpB])
                for gi in range(2 * NKC):
                    S.dma(lambda e, gi=gi: e.dma_start(
                        out=vp[:, :, gi * 128 + 64:gi * 128 + 128],
                        in_=V_s[:, gi * 64:(gi + 1) * 64].rearrange("(kb p) d -> p kb d", p=128)), writes=[vpB])
                qtiles = [tl for tl in tiles if (not tl[2]) or need_ctx]
                for j in range(NQC):
                    c = j // 4 if cfg.split == 1 else 0
                    for (t0, n, isctx) in qtiles:
                        nkb = NCTX // 128 if isctx else NBLK
                        q_, sg_ = qT.next(), sga.next()
                        S.dma(lambda e, q_=q_, j=j, t0=t0, n=n: e.dma_start(out=q_.t[0:64, 0, 0:n], in_=QT_s[j, 0:64, t0:t0 + n]), writes=[q_.b])
                        S.dma(lambda e, q_=q_, j=j, t0=t0, n=n: e.dma_start(out=q_.t[64:128, 1, 0:n], in_=QT_s[j, 64:128, t0:t0 + n]), writes=[q_.b])
                        S.dma(lambda e, sg_=sg_, j=j, t0=t0, n=n: e.dma_start(out=sg_.t[:, 0:n], in_=SGA_s[j, :, t0:t0 + n]), writes=[sg_.b])
                        O = Ops

                        def emitS(kb, q_=q_, n=n, c=c):
                            s_ = Sps.next()
                            p_ = P.next()
                            for hh in range(2):
                                S.op("pe", lambda e, s_=s_, hh=hh, kb=kb: e.matmul(
                                    s_.t[:, hh * 512:hh * 512 + n], lhsT=kt.t[:, c, kb * 128:(kb + 1) * 128],
                                    rhs=q_.t[:, hh, 0:n], start=True, stop=True),
                                    reads=[kt.b, q_.b], writes=[s_.b])
                            if n == 512:
                                S.op("act", lambda e, s_=s_, p_=p_: e.activation(out=p_.t[:, :], in_=s_.t[:, :], func=AF.Exp),
                                     reads=[s_.b], writes=[p_.b])
                            else:
                                for hh in range(2):
                                    S.op("act", lambda e, s_=s_, p_=p_, hh=hh: e.activation(
                                        out=p_.t[:, hh * 512:hh * 512 + n], in_=s_.t[:, hh * 512:hh * 512 + n], func=AF.Exp),
                                        reads=[s_.b], writes=[p_.b])
                            return (s_, p_)

                        def emitPV(kb, sp, n=n, c=c, O=O, nkb=nkb):
                            for hh in range(2):
                                gi = 2 * c + hh
                                col0 = gi * 128 + 64 if hh == 0 else gi * 128
                                p_ = sp[1]
                                S.op("pe", lambda e, hh=hh, p_=p_, col0=col0, kb=kb: e.matmul(
                                    O[hh].t[:, 0:n], lhsT=vp[:, kb, col0:col0 + 128], rhs=p_.t[:, hh * 512:hh * 512 + n],
                                    start=(kb == 0), stop=(kb == nkb - 1)), reads=[vpB, p_.b], writes=[O[hh].b])

                        queue = [emitS(k) for k in range(min(LOOK, nkb))]
                        for kb in range(nkb):
                            if kb + LOOK < nkb:
                                queue.append(emitS(kb + LOOK))
                            emitPV(kb, queue.pop(0))
                        rd_, og_, ot_ = rd.next(), og32.next(), ogt.next()
                        oc = [ocp[0].next(), ocp[1].next()]
                        for hh in range(2):
                            S.op("dve", lambda e, hh=hh, oc=oc, O=O, n=n: e.tensor_copy(out=oc[hh].t[:, 0:n], in_=O[hh].t[:, 0:n]),
                                 reads=[O[hh].b], writes=[oc[hh].b])
                        S.op("dve", lambda e, rd_=rd_, oc=oc, n=n: e.reciprocal(out=rd_.t[0:64, 0:n], in_=oc[0].t[64:128, 0:n]),
                             reads=[oc[0].b], writes=[rd_.b])
                        S.op("dve", lambda e, rd_=rd_, oc=oc, n=n: e.reciprocal(out=rd_.t[64:128, 0:n], in_=oc[1].t[0:64, 0:n]),
                             reads=[oc[1].b], writes=[rd_.b])
                        S.op("pool", lambda e, rd_=rd_, og_=og_, oc=oc, n=n: e.tensor_tensor(
                            out=og_.t[0:64, 0:n], in0=oc[0].t[0:64, 0:n], in1=rd_.t[0:64, 0:n], op=ALU.mult),
                            reads=[oc[0].b, rd_.b], writes=[og_.b])
                        S.op("pool", lambda e, rd_=rd_, og_=og_, oc=oc, n=n: e.tensor_tensor(
                            out=og_.t[64:128, 0:n], in0=oc[1].t[64:128, 0:n], in1=rd_.t[64:128, 0:n], op=ALU.mult),
                            reads=[oc[1].b, rd_.b], writes=[og_.b])
                        S.op("pool", lambda e, og_=og_, ot_=ot_, sg_=sg_, n=n: e.tensor_tensor(
                            out=ot_.t[:, 0:n], in0=og_.t[:, 0:n], in1=sg_.t[:, 0:n], op=ALU.mult),
                            reads=[og_.b, sg_.b], writes=[ot_.b])
                        S.dma(lambda e, ot_=ot_, j=j, t0=t0, n=n: e.dma_start(out=OG_s[j, :, t0:t0 + n], in_=ot_.t[:, 0:n]),
                              reads=[ot_.b], Q="pool")
                S.end_phase()

            with ExitStack() as ph:
                wst = sbn(ph, "wstd", [128, 8, 512], F32, 2)
                wa = sb(ph, "wa", [128, NQC, D], BF16)
                wb = sb(ph, "wb", [128, NLC, D], BF16)
                wo = sb(ph, "wo", [128, 8, D], BF16)
                og = sbn(ph, "og", [128, NQC, 512], BF16, 2)
                hgt = sbn(ph, "hgt", [128, NLC, 512], BF16, 2)
                sgm = sbn(ph, "sgm", [128, 16, 512], BF16, 2)
                mT = sb(ph, "mT", [128, 8, 512], BF16)
                mB = [Buf("mT%d" % i) for i in range(8)]
                y1 = sbn(ph, "y1", [128, 512], F32, 2)
                y2 = sbn(ph, "y2", [128, 512], F32, 2)
                xt = sbn(ph, "xtd", [128, D], F32, 2)
                xn = sbn(ph, "xn", [128, D], F32, 2)
                zz = sbn(ph, "zz", [128, 512], F32, 2)
                pya = psn(ph, "pya", 2)
                pyb = psn(ph, "pyb", 2)
                pz = psn(ph, "pz", 2)
                for (wdst, wsrc, nk) in ((wa, w_a_d, NQC), (wb, w_b_d, NLC), (wo, w_o_d, 8)):
                    for hf in range(2):
                        w = wst.next()
                        S.dma(lambda e, w=w, wsrc=wsrc, hf=hf, nk=nk: e.dma_start(
                            out=w.t[:, 0:nk, :], in_=wsrc[l, :, hf * 512:(hf + 1) * 512].rearrange("(kc p) n -> p kc n", p=128)),
                            writes=[w.b])
                        S.op("act", lambda e, w=w, wdst=wdst, hf=hf, nk=nk: e.activation(
                            out=wdst.t[:, :, hf * 512:(hf + 1) * 512], in_=w.t[:, 0:nk, :], func=AF.Copy), reads=[w.b], writes=[wdst.b])
                dtiles = [tl for tl in tiles if (not tl[2]) or need_ctx]
                for (t0, n, isctx) in dtiles:
                    v = 1 if isctx else 0
                    og_, hg_, sg_ = og.next(), hgt.next(), sgm.next()
                    S.dma(lambda e, og_=og_, t0=t0, n=n: e.dma_start(out=og_.t[:, :, 0:n], in_=OG_s[:, :, t0:t0 + n].rearrange("c p t -> p c t")),
                          writes=[og_.b])
                    S.dma(lambda e, hg_=hg_, t0=t0, n=n: e.dma_start(out=hg_.t[:, :, 0:n], in_=HG_s[:, :, t0:t0 + n].rearrange("c p t -> p c t")),
                          writes=[hg_.b])
                    S.dma(lambda e, sg_=sg_, t0=t0, n=n: e.dma_start(out=sg_.t[:, :, 0:n], in_=SGM_s[:, :, t0:t0 + n].rearrange("c p t -> p c t")),
                          writes=[sg_.b])
                    for fo in range(8):
                        pa, pb = pya.next(), pyb.next()
                        for j in range(NQC):
                            S.op("pe", lambda e, pa=pa, og_=og_, j=j, fo=fo, n=n: e.matmul(
                                pa.t[:, 0:n], lhsT=wa.t[:, j, fo * 128:(fo + 1) * 128], rhs=og_.t[:, j, 0:n],
                                start=(j == 0), stop=(j == NQC - 1)), reads=[wa.b, og_.b], writes=[pa.b])
                        for j in range(NLC):
                            S.op("pe", lambda e, pb=pb, hg_=hg_, j=j, fo=fo, n=n: e.matmul(
                                pb.t[:, 0:n], lhsT=wb.t[:, j, fo * 128:(fo + 1) * 128], rhs=hg_.t[:, j, 0:n],
                                start=(j == 0), stop=(j == NLC - 1)), reads=[wb.b, hg_.b], writes=[pb.b])
                        y1_, y2_ = y1.next(), y2.next()
                        S.op("dve", lambda e, y1_=y1_, pa=pa, sg_=sg_, fo=fo, n=n: e.tensor_tensor(
                            out=y1_.t[:, 0:n], in0=pa.t[:, 0:n], in1=sg_.t[:, fo, 0:n], op=ALU.mult),
                            reads=[pa.b, sg_.b], writes=[y1_.b])
                        S.op("dve", lambda e, y2_=y2_, pb=pb, sg_=sg_, fo=fo, n=n: e.tensor_tensor(
                            out=y2_.t[:, 0:n], in0=pb.t[:, 0:n], in1=sg_.t[:, 8 + fo, 0:n], op=ALU.mult),
                            reads=[pb.b, sg_.b], writes=[y2_.b])
                        S.op("pool", lambda e, y1_=y1_, y2_=y2_, fo=fo, n=n: e.tensor_tensor(
                            out=mT.t[:, fo, 0:n], in0=y1_.t[:, 0:n], in1=y2_.t[:, 0:n], op=ALU.add),
                            reads=[y1_.b, y2_.b], writes=[mB[fo]])
                    for tb in range(n // 128):
                        r0 = t0 + tb * 128
                        if cfg.split == 1:
                            x_, xn_ = xt.next(), xn.next()
                            S.dma(lambda e, x_=x_, r0=r0: e.dma_start(out=x_.t[:], in_=xsrc[r0:r0 + 128, :]), writes=[x_.b])
                        zb = []
                        for fh in range(2):
                            pz_ = pz.next()
                            for k in range(8):
                                S.op("pe", lambda e, pz_=pz_, k=k, tb=tb, fh=fh: e.matmul(
                                    pz_.t[:], lhsT=mT.t[:, k, tb * 128:(tb + 1) * 128], rhs=wo.t[:, k, fh * 512:(fh + 1) * 512],
                                    start=(k == 0), stop=(k == 7)), reads=[mB[k], wo.b], writes=[pz_.b])
                            z_ = zz.next()
                            if cfg.split == 1:
                                S.op("dve", lambda e, z_=z_, pz_=pz_, fh=fh, v=v: e.tensor_tensor(
                                    out=z_.t[:], in0=pz_.t[:], in1=gtb[v].t[:, fh * 512:(fh + 1) * 512], op=ALU.mult),
                                    reads=[pz_.b, gtb[v].b], writes=[z_.b])
                                S.op("pool", lambda e, z_=z_, x_=x_, xn_=xn_, fh=fh: e.tensor_tensor(
                                    out=xn_.t[:, fh * 512:(fh + 1) * 512], in0=z_.t[:], in1=x_.t[:, fh * 512:(fh + 1) * 512], op=ALU.add),
                                    reads=[z_.b, x_.b], writes=[xn_.b])
                            else:
                                S.op("act", lambda e, z_=z_, pz_=pz_: e.activation(out=z_.t[:], in_=pz_.t[:], func=AF.Copy),
                                     reads=[pz_.b], writes=[z_.b])
                                zB = Buf("zp")
                                zb.append(zB)
                                S.dma(lambda e, z_=z_, r0=r0, fh=fh: e.dma_start(
                                    out=Zp_s[r0:r0 + 128, fh * 512:(fh + 1) * 512], in_=z_.t[:]), reads=[z_.b], dram_writes=[zB], Q="act")
                        if cfg.split == 1:
                            if last:
                                dst = out_d[r0 - NCTX:r0 - NCTX + 128, :]
                            else:
                                dst = X_s[r0:r0 + 128, :]
                            S.dma(lambda e, xn_=xn_, dst=dst: e.dma_start(out=dst, in_=xn_.t[:]), reads=[xn_.b], Q="pool")
                        else:
                            S.coll(lambda e, r0=r0: e.collective_compute(
                                "AllGather", ALU.bypass, replica_groups=PAIRS,
                                ins=[Zp_s[r0:r0 + 128, :]], outs=[Zr_s[r0 // 128]]), reads=zb,
                                writes=[ZrB[r0 // 128]], key=r0 // 128)
                if l + 1 < L and cfg.split == 1:
                    for _ in emit_M(l + 1, ph, pgt_slots=1):
                        pass
                S.end_phase(skip_cc=(cfg.split == 2))
            if cfg.split == 2:
                with ExitStack() as ph:
                    xt = sbn(ph, "xte", [128, D], F32, 3)
                    zr = sbn(ph, "zre", [128, 2, D], F32, 3)
                    xn = sbn(ph, "xne", [128, D], F32, 3)
                    genE = emit_M(l + 1, ph) if l + 1 < L else None
                    for tb in range(NBLK):
                        if genE is not None and tb % 5 == 4:
                            next(genE, None)
                        isctx = tb < NCTX // 128
                        if isctx and not need_ctx:
                            continue
                        v = 1 if isctx else 0
                        r0 = tb * 128
                        x_, z_, xn_ = xt.next(), zr.next(), xn.next()
                        S.dma(lambda e, x_=x_, r0=r0: e.dma_start(out=x_.t[:], in_=xsrc[r0:r0 + 128, :]), writes=[x_.b])
                        S.dma(lambda e, z_=z_, tb=tb: e.dma_start(out=z_.t[:], in_=Zr_s[tb].rearrange("(r p) n -> p r n", p=128)),
                              writes=[z_.b], dram_reads=[ZrB[tb]])
                        S.op("dve", lambda e, z_=z_: e.tensor_tensor(out=z_.t[:, 0, :], in0=z_.t[:, 0, :], in1=z_.t[:, 1, :], op=ALU.add),
                             reads=[z_.b], writes=[z_.b])
                        S.op("dve", lambda e, z_=z_, v=v: e.tensor_tensor(out=z_.t[:, 0, :], in0=z_.t[:, 0, :], in1=gtb[v].t[:], op=ALU.mult),
                             reads=[z_.b, gtb[v].b], writes=[z_.b])
                        S.op("pool", lambda e, z_=z_, x_=x_, xn_=xn_: e.tensor_tensor(out=xn_.t[:], in0=z_.t[:, 0, :], in1=x_.t[:], op=ALU.add),
                             reads=[z_.b, x_.b], writes=[xn_.b])
                        if last:
                            dst = out_d[r0 - NCTX:r0 - NCTX + 128, :]
                        else:
                            dst = X_s[r0:r0 + 128, :]
                        S.dma(lambda e, xn_=xn_, dst=dst: e.dma_start(out=dst, in_=xn_.t[:]), reads=[xn_.b], Q="pool")
                    if genE is not None:
                        for _ in genE:
                            pass
                    S.end_phase()
        print("instructions recorded:", S.n_instr, "dma sems:", S.n_dsem, "sbuf left:", nc.sbuf_bytes_remaining)
    return nc


_CACHE = {}


def run(cfg, inputs):
    key = (cfg.L, cfg.split, cfg.dbg, cfg.total_depth)
    if key not in _CACHE:
        _CACHE[key] = build_program(cfg)
    nc = _CACHE[key]
    in_maps = []
    for core in range(8):
        if cfg.split == 1:
            b, half = core % 4, 0
        else:
            b, half = core // 2, core % 2
        in_maps.append(prep_core_inputs(cfg, inputs, b, half))
    res = run_bass_kernel_spmd(nc, in_maps, core_ids=list(range(8)))
    return res


SPLIT = 2


def kernel(**inputs):
    cfg = Cfg(L=DEPTH, split=SPLIT)
    res = run(cfg, inputs)
    out = np.stack([np.asarray(res.results[b * SPLIT]["out"]) for b in range(4)], axis=0)
    return out.astype(np.float32)
```
